# Optimizing a Trainium2 kernel written in Bass

```python
import math
import jax, jax.numpy as jnp
from jax import lax
import numpy as np

D_MODEL = 1024
BATCH = 4
SEQ = 8192
DEPTH = 2

N_A = DEPTH // 2
N_B = DEPTH - N_A

CONV_CH = D_MODEL
CONV_W = 3

HEAD_DIM = 64
N_HEADS = D_MODEL // HEAD_DIM
N_KV_HEADS = 4
GROUP = N_HEADS // N_KV_HEADS
WINDOW = 128
BLOCK = 128
ROT_DIM = HEAD_DIM // 4
ROPE_THETA = 500000.0
EPS = 1e-6
NEG_INF = -1e30

kernel_name = "yoco_shortconv_swa_sink_hybrid"


def rmsnorm(x, g):
    xf = x.astype(jnp.float32)
    y = xf * lax.rsqrt(jnp.mean(xf * xf, axis=-1, keepdims=True) + EPS)
    return (y * g.astype(jnp.float32)).astype(x.dtype)


def rope_tables(seq):
    pos = jnp.arange(seq, dtype=jnp.float32)
    inv = ROPE_THETA ** (-jnp.arange(0, ROT_DIM, 2, dtype=jnp.float32) / ROT_DIM)
    ang = pos[:, None] * inv[None, :]
    return jnp.cos(ang), jnp.sin(ang)


def partial_rope(t, cos, sin):
    tf = t.astype(jnp.float32)
    rot, rest = tf[..., :ROT_DIM], tf[..., ROT_DIM:]
    r1, r2 = rot[..., : ROT_DIM // 2], rot[..., ROT_DIM // 2:]
    c = cos[None, :, None, :]
    s = sin[None, :, None, :]
    out = jnp.concatenate([r1 * c - r2 * s, r2 * c + r1 * s, rest], axis=-1)
    return out.astype(t.dtype)


def short_conv_mixer(x, ln_g, w_in, conv_w, w_out):
    h = rmsnorm(x, ln_g)
    b_gate, c_gate, u, z = jnp.split(h @ w_in, 4, axis=-1)
    v = c_gate * u
    conv = lax.conv_general_dilated(
        v, conv_w.astype(v.dtype), window_strides=(1,), padding=[(CONV_W - 1, 0)],
        dimension_numbers=("NWC", "WIO", "NWC"), feature_group_count=CONV_CH)
    y = b_gate * conv * jax.nn.silu(z)
    return y @ w_out


def shared_kv(x, ln_g, w_kv, k_norm_g, cos, sin):
    bsz, seq, _ = x.shape
    nb = seq // BLOCK
    h = rmsnorm(x, ln_g)
    k, v = jnp.split(h @ w_kv, 2, axis=-1)
    k = k.reshape(bsz, seq, N_KV_HEADS, HEAD_DIM)
    v = v.reshape(bsz, seq, N_KV_HEADS, HEAD_DIM)
    k = partial_rope(rmsnorm(k, k_norm_g), cos, sin)

    def band(t):
        tb = t.reshape(bsz, nb, BLOCK, N_KV_HEADS, HEAD_DIM)
        prev = jnp.concatenate([jnp.zeros_like(tb[:, :1]), tb[:, :-1]], axis=1)
        tb = jnp.concatenate([prev, tb], axis=2)
        return jnp.moveaxis(tb, 1, 0)

    return band(k), band(v)


def swa_sink_mixer(x, ln_g, w_in, q_norm_g, sinks, w_out, k_band, v_band, cos, sin):
    bsz, seq, _ = x.shape
    nb = seq // BLOCK
    h = rmsnorm(x, ln_g)
    q, z = jnp.split(h @ w_in, 2, axis=-1)
    q = q.reshape(bsz, seq, N_HEADS, HEAD_DIM)
    q = partial_rope(rmsnorm(q, q_norm_g), cos, sin)
    q = q.reshape(bsz, nb, BLOCK, N_KV_HEADS, GROUP, HEAD_DIM)
    q = jnp.moveaxis(q, 1, 0)

    scale = 1.0 / math.sqrt(HEAD_DIM)
    a_idx = jnp.arange(BLOCK)[:, None]
    c_idx = jnp.arange(2 * BLOCK)[None, :]
    rel_ok = (c_idx <= a_idx + BLOCK) & (c_idx > a_idx + BLOCK - WINDOW)
    sink = sinks.astype(jnp.float32).reshape(N_KV_HEADS, GROUP)[None, :, :, None, None]

    def block_attn(args):
        qb, kb, vb, blk = args
        s = jnp.einsum("bqkgd,bckd->bkgqc", qb.astype(jnp.float32),
                       kb.astype(jnp.float32)) * scale
        mask = rel_ok & ((blk > 0) | (c_idx >= BLOCK))
        s = jnp.where(mask[None, None, None], s, NEG_INF)
        m = jnp.maximum(jnp.max(s, axis=-1, keepdims=True), sink)
        p = jnp.exp(s - m)
        denom = jnp.sum(p, axis=-1, keepdims=True) + jnp.exp(sink - m)
        o = jnp.einsum("bkgqc,bckd->bqkgd", p / denom, vb.astype(jnp.float32))
        return o.astype(x.dtype)

    o = lax.map(block_attn, (q, k_band, v_band, jnp.arange(nb)))
    o = jnp.moveaxis(o, 0, 1).reshape(bsz, seq, N_HEADS * HEAD_DIM)
    return (o * jax.nn.silu(z)) @ w_out


def setup_inputs(seed: int = 0) -> dict:
    key = jax.random.key(seed)
    ks = jax.random.split(key, 16)
    f32 = jnp.float32
    D, W, H, KV, HD = D_MODEL, CONV_CH, N_HEADS, N_KV_HEADS, HEAD_DIM
    nrm = lambda k, shape, fan: jax.random.normal(k, shape, f32) * (fan ** -0.5)
    return {
        "x": jax.random.normal(ks[0], (BATCH, SEQ, D), f32),
        "ln_a": 1.0 + 0.02 * jax.random.normal(ks[1], (N_A, D), f32),
        "w_in_a": nrm(ks[2], (N_A, D, 4 * W), D),
        "conv_a": nrm(ks[3], (N_A, CONV_W, 1, W), CONV_W),
        "w_out_a": nrm(ks[4], (N_A, W, D), W),
        "ln_kv": 1.0 + 0.02 * jax.random.normal(ks[5], (D,), f32),
        "w_kv": nrm(ks[6], (D, 2 * KV * HD), D),
        "k_norm": 1.0 + 0.02 * jax.random.normal(ks[7], (HD,), f32),
        "ln_b": 1.0 + 0.02 * jax.random.normal(ks[8], (N_B, D), f32),
        "w_in_b": nrm(ks[9], (N_B, D, 2 * H * HD), D),
        "q_norm": 1.0 + 0.02 * jax.random.normal(ks[10], (N_B, HD), f32),
        "sinks": 0.5 * jax.random.normal(ks[11], (N_B, H), f32),
        "w_out_b": nrm(ks[12], (N_B, H * HD, D), H * HD),
    }


def reference(x, ln_a, w_in_a, conv_a, w_out_a, ln_kv, w_kv, k_norm,
              ln_b, w_in_b, q_norm, sinks, w_out_b):
    seq = x.shape[1]
    cos, sin = rope_tables(seq)
    k_band = v_band = None
    for layer in range(DEPTH):
        if layer < N_A:
            i = layer
            x = x + short_conv_mixer(x, ln_a[i], w_in_a[i], conv_a[i], w_out_a[i])
        else:
            i = layer - N_A
            if i == 0:
                k_band, v_band = shared_kv(x, ln_kv, w_kv, k_norm, cos, sin)
            x = x + swa_sink_mixer(x, ln_b[i], w_in_b[i], q_norm[i], sinks[i], w_out_b[i],
                                   k_band, v_band, cos, sin)
    return x
```

```python
import contextlib
import numpy as np
import concourse.bass as bass
import concourse.mybir as mybir
from concourse.bass_utils import run_bass_kernel_spmd

F32 = mybir.dt.float32
BF16 = mybir.dt.bfloat16
AF = mybir.ActivationFunctionType
ALU = mybir.AluOpType

D = 1024
NCORE = 8
SEQ = 8192
TOK_CORE = 4096
HALO = 256
T = 512
EPS = 1e-6
ROPE_THETA = 500000.0
ENGS = ("pe", "act", "dve", "pool", "sp")
STORE_Q = "pool"
import os
FULL_DEPS = os.environ.get("K_FULL_DEPS", "1") == "1"
F_SINK2 = os.environ.get("K_SINK2", "1") == "1"
F_S256 = os.environ.get("K_S256", "1") == "1"
F_MASKDVE = os.environ.get("K_MASKDVE", "1") == "1"


class Prog:
    def __init__(self, nc):
        self.nc = nc
        self.ops = []
        self.eng_ops = {e: [] for e in ENGS}
        self.last_w = {}
        self.readers = {}
        self.dma_cnt = {}

    def _add(self, eng, fn, reads, writes, dma_key=None, ndma=1):
        idx = len(self.ops)
        deps = set()
        for r in reads:
            w = self.last_w.get(r)
            if w is not None:
                deps.add(w)
            if isinstance(r, tuple) and r[0] == "ps":
                for rd in self.readers.get(r, ()):
                    if self.ops[rd]["eng"] != eng:
                        deps.add(rd)
        for r in writes:
            w = self.last_w.get(r)
            if w is not None:
                deps.add(w)
            for rd in self.readers.get(r, ()):
                deps.add(rd)
        op = dict(idx=idx, eng=eng, fn=fn, dma_key=dma_key, ndma=ndma, signal=False)
        prev = self.eng_ops[eng][-1] if self.eng_ops[eng] else None
        know = dict(prev["know"]) if prev is not None else {}
        kept = []
        for d in sorted(deps, reverse=True):
            p = self.ops[d]
            src, seq = p["src"], p["seq"]
            if know.get(src, 0) >= seq:
                continue
            kept.append(d)
            for k2, v2 in p["know"].items():
                if know.get(k2, 0) < v2:
                    know[k2] = v2
            know[src] = seq
        op["deps"] = kept
        op["know"] = know
        if dma_key is not None:
            op["src"] = ("d", dma_key)
            op["seq"] = self.dma_cnt.get(dma_key, 0) + 16 * ndma
        else:
            op["src"] = ("e", eng)
            op["seq"] = len(self.eng_ops[eng]) + 1
        if dma_key is not None:
            self.dma_cnt[dma_key] = self.dma_cnt.get(dma_key, 0) + 16 * ndma
            op["dma_count"] = self.dma_cnt[dma_key]
        self.ops.append(op)
        self.eng_ops[eng].append(op)
        for r in reads:
            self.readers.setdefault(r, []).append(idx)
        for r in writes:
            self.last_w[r] = idx
            self.readers[r] = []
        return idx

    def op(self, eng, fn, reads=(), writes=()):
        return self._add(eng, fn, tuple(reads), tuple(writes))

    def dma(self, eng, fn, key, ndma=1, reads=(), writes=()):
        return self._add(eng, fn, tuple(reads), tuple(writes), dma_key=key, ndma=ndma)

    def emit(self):
        nc = self.nc
        ops = self.ops
        for op in ops:
            for d in op["deps"]:
                if ops[d]["dma_key"] is None:
                    ops[d]["signal"] = True
        cnt = {e: 0 for e in ENGS}
        for e in ENGS:
            for op in self.eng_ops[e]:
                if op["dma_key"] is None and op["signal"]:
                    cnt[e] += 1
                    op["count"] = cnt[e]
        dma_keys = sorted(self.dma_cnt.keys(), key=str)
        nwait = {e: 0 for e in ENGS}
        with contextlib.ExitStack() as st:
            esem = {e: st.enter_context(nc.semaphore("s_" + e)) for e in ENGS}
            dsem = {k: st.enter_context(nc.semaphore("d_%d" % i)) for i, k in enumerate(dma_keys)}
            block = st.enter_context(nc.Block())

            def make(e):
                def body(eng):
                    waited = {}
                    for op in self.eng_ops[e]:
                        for d in op["deps"]:
                            p = ops[d]
                            if p["dma_key"] is not None:
                                key = ("d", p["dma_key"])
                                val = p["dma_count"]
                                sem = dsem[p["dma_key"]]
                            else:
                                key = ("e", p["eng"])
                                val = p["count"]
                                sem = esem[p["eng"]]
                            if waited.get(key, 0) >= val:
                                continue
                            waited[key] = val
                            eng.wait_ge(sem, val)
                            nwait[e] += 1
                        r = op["fn"](eng)
                        if op["dma_key"] is not None:
                            insts = r if isinstance(r, (list, tuple)) else [r]
                            assert len(insts) == op["ndma"], (len(insts), op["ndma"])
                            for i in insts:
                                i.then_inc(dsem[op["dma_key"]], 16)
                        elif op["signal"]:
                            r.then_inc(esem[e], 1)
                    if e == "sp":
                        for k in dma_keys:
                            eng.wait_ge(dsem[k], self.dma_cnt[k])
                return body

            block.tensor(make("pe"))
            block.scalar(make("act"))
            block.vector(make("dve"))
            block.gpsimd(make("pool"))
            block.sync(make("sp"))
        return cnt, nwait


PID_AIN = list(range(0, 8))
PID_AOUT = [8, 9]
PID_V = 10
PID_K = 10
PID_BQ = [11, 12]
PID_BZ = [13, 14]
PID_BOUT = [15, 16]
NPIECE = 17


def build_nc(n_chunks=8, dbg=False):
    nc = bass.Bass("TRN2", target_bir_lowering=False)
    TT = HALO + n_chunks * T

    def din(name, shape, dt=F32):
        return nc.dram_tensor(name, shape, dt, kind="ExternalInput").ap()

    x_d = din("x", [TT, D])
    w_in_a = din("w_in_a", [D, 4096])
    w_out_a = din("w_out_a", [D, D])
    w_kv = din("w_kv", [D, 512])
    w_in_b = din("w_in_b", [D, 2048])
    w_out_b = din("w_out_b", [D, D])
    gains_d = din("gains", [128, 24])
    convw_d = din("convw", [128, 24])
    qkg_d = din("qkg", [128, 4])
    sink_d = din("sink2", [2, 4 * 128])
    cst_d = din("cst", [128, 2, TT])
    masks_d = din("masks", [128, 4, 128])
    cmat_d = din("cmat", [128, 6, 128])
    sel_d = din("sel", [2, 256])
    out_d = nc.dram_tensor("out", [n_chunks * T, D], F32, kind="ExternalOutput").ap()
    wscr = nc.dram_tensor("wscr", [NPIECE, 128, 4096], BF16, kind="Internal").ap()
    if dbg:
        dbg_x1 = nc.dram_tensor("dbg_x1", [T, D], F32, kind="ExternalOutput").ap()
        dbg_kt = nc.dram_tensor("dbg_kt", [128, 4 * 2 * T], BF16, kind="ExternalOutput").ap()
        dbg_qt = nc.dram_tensor("dbg_qt", [128, 8 * T], BF16, kind="ExternalOutput").ap()
        dbg_yt = nc.dram_tensor("dbg_yt", [128, 8 * T], BF16, kind="ExternalOutput").ap()

    with contextlib.ExitStack() as st:
        def sb(name, shape, dt):
            return st.enter_context(nc.sbuf_tensor(name, shape, dt))

        xb = [sb("xb%d" % i, [128, 4, D], F32) for i in range(2)]
        wrt = sb("wr", [128, 4, 4096], BF16)
        WR = lambda slot: wrt[:, slot, :]
        junk = sb("junk", [128, 2, D], BF16)
        xs = [sb("xs%d" % i, [128, D], BF16) for i in range(2)]
        hty = sb("hty", [128, 4, 8 * T], BF16)
        hT = [hty[:, i, :].rearrange("p (a b) -> p a b", a=8) for i in range(3)]
        NSC = 12
        sc = [sb("sc%d" % i, [128, 520], F32) for i in range(NSC)]
        qs = [sb("qs%d" % i, [128, 2, T], BF16) for i in range(2)]
        yT = hty[:, 3, :].rearrange("p (a b) -> p a b", a=8)
        KTz = sb("KTz", [128, 2, 4, 2, T], BF16)
        Vp = [sb("Vp%d" % i, [128, 4, 2, 128], BF16) for i in range(8)]
        QT = sb("QT", [128, 8, T], BF16)
        cs = [sb("cs%d" % i, [128, 2, T], F32) for i in range(2)]
        szT = sb("szT", [128, 8, T], BF16)
        PT = [sb("PT%d" % i, [128, 2, 512], BF16) for i in range(2)]
        ssq = sb("ssq", [128, 4], F32)
        rt4 = sb("rt4", [128, 4], F32)
        rstd4 = sb("rstd4", [128, 4], F32)
        gains = sb("gains_s", [128, 24], F32)
        convw = sb("convw_s", [128, 24], F32)
        qkg = sb("qkg_s", [128, 4], F32)
        vhalo = sb("vhalo", [128, 8, 2], F32)
        cmat_f = sb("cmat_f", [128, 6, 128], F32)
        cmat_b = sb("cmat_b", [128, 6, 128], BF16)
        masks_f = sb("masks_f", [128, 4, 128], F32)
        masks_b = sb("masks_b", [128, 4, 128], BF16)
        sink_f = sb("sink_f", [2, 512], F32)
        sink_b = sb("sink_b", [2, 512], BF16)
        sel_f = sb("sel_f", [2, 256], F32)
        sel_b = sb("sel_b", [2, 256], BF16)
        sinkmat = sb("sinkmat", [128, 512], BF16)
        selmat = sb("selmat", [128, 256], BF16)
        ps = [st.enter_context(nc.psum_tensor("ps%d" % i, [128, 512], F32)) for i in range(8)]

        ident = cmat_b[:, 0, :]
        bdm = cmat_b[:, 1, :]
        rotm = cmat_b[:, 2, :]
        swapm = cmat_b[:, 5, :]

        P = Prog(nc)
        XB = lambda b: [("xb", b, t) for t in range(4)]
        HT = lambda h: [k_ for t in range(4) for k_ in (("hT", h, t), ("hT", h, t, "b"))]

        P.dma("sp", lambda e: [
            e.dma_start(out=gains[:], in_=gains_d), e.dma_start(out=convw[:], in_=convw_d),
            e.dma_start(out=qkg[:], in_=qkg_d), e.dma_start(out=sink_f[:], in_=sink_d),
            e.dma_start(out=masks_f[:], in_=masks_d), e.dma_start(out=cmat_f[:], in_=cmat_d),
            e.dma_start(out=sel_f[:], in_=sel_d)],
            "const", ndma=7, writes=["gains", "convw", "qkg", "sink_f", "masks_f", "cmat_f", "sel_f"])
        P.op("dve", lambda e: e.tensor_copy(out=cmat_b[:], in_=cmat_f[:]), reads=["cmat_f"], writes=["cmat_b"])
        P.op("dve", lambda e: e.tensor_copy(out=masks_b[:], in_=masks_f[:]), reads=["masks_f"], writes=["masks_b"])
        P.op("dve", lambda e: e.tensor_copy(out=sel_b[:], in_=sel_f[:]), reads=["sel_f"], writes=["sel_b"])
        P.op("act", lambda e: e.activation(out=sink_b[:], in_=sink_f[:], func=AF.Exp), reads=["sink_f"], writes=["sink_b"])
        P.op("pool", lambda e: e.memset(sinkmat[:], 0.0), writes=["sinkmat"])
        P.op("pool", lambda e: e.memset(selmat[:], 0.0), writes=["selmat"])
        P.op("dve", lambda e: e.tensor_copy(out=sinkmat[0:2, :], in_=sink_b[:]), reads=["sink_b", "sinkmat"], writes=["sinkmat"])
        P.op("dve", lambda e: e.tensor_copy(out=selmat[0:2, :], in_=sel_b[:]), reads=["sel_b", "selmat"], writes=["selmat"])
        P.op("pool", lambda e: e.memset(KTz[:], 0.0), writes=[("KT", s_, k_) for s_ in range(2) for k_ in range(4)])
        P.op("pool", lambda e: e.memset(vhalo[:], 0.0), writes=[("vhalo", j) for j in range(8)])
        for i in range(8):
            P.op("pool", (lambda i: lambda e: e.memset(Vp[i][:], 0.0))(i), writes=[("Vp", i)])

        rr = [0]

        def conv_op(out_ap, in_ap, gain_ap, reads, writes):
            engs = ("act", "dve", "pool")
            eng = engs[rr[0] % len(engs)]
            rr[0] += 1
            if eng == "act":
                if gain_ap is None:
                    fn = lambda e: e.activation(out=out_ap, in_=in_ap, func=AF.Copy)
                else:
                    fn = lambda e: e.activation(out=out_ap, in_=in_ap, func=AF.Copy, scale=gain_ap)
            else:
                if gain_ap is None:
                    fn = lambda e: e.tensor_copy(out=out_ap, in_=in_ap)
                else:
                    fn = lambda e: e.tensor_scalar(out=out_ap, in0=in_ap, scalar1=gain_ap, scalar2=1.0,
                                                   op0=ALU.mult, op1=ALU.mult)
            P.op(eng, fn, reads=reads, writes=writes)

        wia = w_in_a.rearrange("(kc p) f -> p kc f", p=128)
        woa = w_out_a.rearrange("(kc p) f -> p kc f", p=128)
        wkv = w_kv.rearrange("(kc p) f -> p kc f", p=128)
        wib = w_in_b.rearrange("(kc p) f -> p kc f", p=128)
        wob = w_out_b.rearrange("(kc p) f -> p kc f", p=128)

        units = []

        ALTK = [k_ for h_ in range(3) for k_ in HT(h_)] + [("yT", j_, t_) for j_ in range(8) for t_ in range(4)]

        def store_piece(pid, slot):
            if slot < 4:
                P.dma("sp", lambda e: [e.dma_start(out=wscr[pid], in_=WR(slot))],
                      ("wst", slot), reads=[("wr", slot)], writes=[("wscr", pid)])
            else:
                P.dma("sp", lambda e: [e.dma_start(out=wscr[pid], in_=hty[:, slot - 4, :])],
                      ("wst", slot), reads=ALTK, writes=[("wscr", pid)])

        for jq in range(2):
            for g in range(4):
                for kh in range(2):
                    def ld(stg, jq=jq, g=g, kh=kh):
                        stg3 = stg.rearrange("p (a b) -> p a b", a=4)
                        c0 = g * 1024 + jq * 512
                        return lambda e: [e.dma_start(out=stg3, in_=wia[:, 4 * kh:4 * kh + 4, c0:c0 + 512])]

                    def cv(stg, sk, jq=jq, g=g, kh=kh):
                        stg3 = stg.rearrange("p (a b) -> p a b", a=4)
                        dst_t = wrt if jq == 0 else hty
                        wkeys = [("wr", q) for q in range(4)] if jq == 0 else ALTK
                        for k4 in range(4):
                            kc = 4 * kh + k4
                            out_ap = dst_t[:, :, kc * 512 + g * 128: kc * 512 + (g + 1) * 128]
                            in_ap = stg3[:, k4, :].rearrange("p (j m) -> p j m", j=4)
                            conv_op(out_ap, in_ap, gains[:, kc:kc + 1], sk + ["gains"], wkeys)
                    stores = [(4 * jq + jj, jj + 4 * jq) for jj in range(4)] if (g == 3 and kh == 1) else []
                    units.append((ld, cv, stores))
        grp_out = {}
        for pid in PID_AOUT + PID_BOUT:
            src = woa if pid in PID_AOUT else wob
            h = pid - (PID_AOUT[0] if pid in PID_AOUT else PID_BOUT[0])
            slot = pid % 4
            lst = []
            for hh in range(2):
                def ld(stg, src=src, h=h, hh=hh):
                    stg3 = stg.rearrange("p (a b) -> p a b", a=2)
                    return lambda e: [e.dma_start(out=stg3, in_=src[:, 4 * h + 2 * hh:4 * h + 2 * hh + 2, :])]

                def cv(stg, sk, slot=slot, hh=hh):
                    stg3 = stg.rearrange("p (a b) -> p a b", a=2)
                    dst3 = WR(slot).rearrange("p (a b) -> p a b", a=4)
                    for a in range(2):
                        conv_op(dst3[:, 2 * hh + a, :], stg3[:, a, :], None, sk, [("wr", slot)])
                lst.append((ld, cv, [(pid, slot)] if hh == 1 else []))
            grp_out[pid] = lst
        kv_units = []
        for kh in range(2):
            def ld_kv(stg, kh=kh):
                stg3 = stg.rearrange("p (a b) -> p a b", a=4)
                return lambda e: [e.dma_start(out=stg3, in_=wkv[:, 4 * kh:4 * kh + 4, :])]

            def cv_kv(stg, sk, kh=kh):
                stg3 = stg.rearrange("p (a b) -> p a b", a=4)
                sv = PID_V % 4
                dstv = WR(sv)[:, 0:2048].rearrange("p (a b) -> p a b", a=8)
                dstk = WR(sv)[:, 2048:4096].rearrange("p (a b) -> p a b", a=8)
                for k4 in range(4):
                    kc = 4 * kh + k4
                    conv_op(dstv[:, kc, :], stg3[:, k4, 256:512], gains[:, 8 + kc:9 + kc], sk + ["gains"], [("wr", sv)])
                    conv_op(dstk[:, kc, :], stg3[:, k4, 0:256], gains[:, 8 + kc:9 + kc], sk + ["gains"], [("wr", sv)])
            kv_units.append((ld_kv, cv_kv, [(PID_V, PID_V % 4)] if kh == 1 else []))
        grp_b = {}
        for pid in PID_BQ + PID_BZ:
            c0 = (pid - PID_BQ[0]) * 512 if pid in PID_BQ else 1024 + (pid - PID_BZ[0]) * 512
            slot = pid % 4
            lst = []
            for kh in range(2):
                def ld(stg, c0=c0, kh=kh):
                    stg3 = stg.rearrange("p (a b) -> p a b", a=4)
                    return lambda e: [e.dma_start(out=stg3, in_=wib[:, 4 * kh:4 * kh + 4, c0:c0 + 512])]

                def cv(stg, sk, slot=slot, kh=kh):
                    stg3 = stg.rearrange("p (a b) -> p a b", a=4)
                    dst3 = WR(slot).rearrange("p (a b) -> p a b", a=8)
                    for k4 in range(4):
                        kc = 4 * kh + k4
                        conv_op(dst3[:, kc, :], stg3[:, k4, :], gains[:, 16 + kc:17 + kc], sk + ["gains"], [("wr", slot)])
                lst.append((ld, cv, [(pid, slot)] if kh == 1 else []))
            grp_b[pid] = lst
        for pid in PID_AOUT:
            units += grp_out[pid]
        units += kv_units

        bgu = []
        ptflat = [PT[a][:].rearrange("p a b -> p (a b)") for a in range(2)]
        for pid in PID_BQ + PID_BZ:
            for kh in range(2):
                bgu.append(("in", None, pid, None, kh))
        for pid in PID_BOUT:
            for hh in range(2):
                bgu.append(("out", wob, pid, pid - PID_BOUT[0], hh))
        bg_stage = [QT[:].rearrange("p a b -> p (a b)").bitcast(F32), szT[:].rearrange("p a b -> p (a b)").bitcast(F32)]
        bg_skeys = [[("QT", c) for c in range(8)], [("szT", c) for c in range(8)]]
        PTK = lambda a: [("PT", a, 0), ("PT", a, 1)]

        def bg_load(k):
            kind, src, pid, h, hh = bgu[k]
            stg = bg_stage[k % 2]
            if kind == "out":
                stg3 = stg.rearrange("p (a b) -> p a b", a=2)
                fn = lambda e: [e.dma_start(out=stg3, in_=src[:, 4 * h + 2 * hh:4 * h + 2 * hh + 2, :])]
            else:
                c0 = (pid - PID_BQ[0]) * 512 if pid in PID_BQ else 1024 + (pid - PID_BZ[0]) * 512
                stg3 = stg.rearrange("p (a b) -> p a b", a=4)
                fn = lambda e: [e.dma_start(out=stg3, in_=wib[:, 4 * hh:4 * hh + 4, c0:c0 + 512])]
            P.dma("sp", fn, ("bgl", k % 2), writes=bg_skeys[k % 2])

        def bg_step(k):
            if k >= len(bgu):
                return
            if k + 1 < len(bgu):
                bg_load(k + 1)
            kind, src, pid, h, hh = bgu[k]
            stg = bg_stage[k % 2]
            sk = bg_skeys[k % 2]
            if kind == "out":
                stg3 = stg.rearrange("p (a b) -> p a b", a=2)
                for a in range(2):
                    for half in range(2):
                        conv_op(PT[a][:, half, :], stg3[:, a, half * 512:(half + 1) * 512], None, sk, PTK(a))
                offs = [(2 * hh + a) * 1024 for a in range(2)]
            else:
                stg3 = stg.rearrange("p (a b) -> p a b", a=4)
                for k4 in range(4):
                    kc = 4 * hh + k4
                    conv_op(PT[k4 // 2][:, k4 % 2, :], stg3[:, k4, :], gains[:, 16 + kc:17 + kc], sk + ["gains"], PTK(k4 // 2))
                offs = [(4 * hh + 2 * a) * 512 for a in range(2)]
            P.dma("sp", lambda e: [e.dma_start(out=wscr[pid][:, offs[a]:offs[a] + 1024], in_=ptflat[a]) for a in range(2)],
                  ("bgs",), ndma=2, reads=PTK(0) + PTK(1), writes=[("wscr", pid)])

        NSTG = 6

        def stg_of(k):
            q = k % NSTG
            if q < 4:
                return xb[q // 2][:, 2 * (q % 2):2 * (q % 2) + 2, :].rearrange("p a b -> p (a b)")
            return bg_stage[q - 4]

        def stg_keys(k):
            q = k % NSTG
            if q < 4:
                return [("xb", q // 2, 2 * (q % 2)), ("xb", q // 2, 2 * (q % 2) + 1)]
            return bg_skeys[q - 4]

        def rec_load(k):
            P.dma("sp", units[k][0](stg_of(k)), ("stg", k % NSTG), writes=stg_keys(k))

        AHEAD = 5
        for k in range(min(AHEAD, len(units))):
            rec_load(k)
        for k in range(len(units)):
            if k + AHEAD < len(units):
                rec_load(k + AHEAD)
            units[k][1](stg_of(k), stg_keys(k))
            for (pid, slot) in units[k][2]:
                store_piece(pid, slot)

        uses = []
        for ci in range(n_chunks + 1):
            npc = 11 if ci == 0 else NPIECE
            for pid in range(npc):
                uses.append((ci, pid))
        use_index = {u: k for k, u in enumerate(uses)}
        nl = [0]

        def need(ci, pid):
            k = use_index[(ci, pid)]
            while nl[0] <= min(k + 3, len(uses) - 1):
                kk = nl[0]
                p2 = uses[kk][1]
                slot = kk % 4
                P.dma("sp", (lambda p2, slot: lambda e: [e.dma_start(out=WR(slot), in_=wscr[p2])])(p2, slot),
                      ("wld", slot), reads=[("wscr", p2)], writes=[("wr", slot)])
                nl[0] += 1
            return k % 4

        def chunk_geom(ci):
            if ci == 0:
                return 0, HALO
            return HALO + (ci - 1) * T, T

        def kslot(blk):
            if blk < 2:
                return 0, blk * 128
            ci = 1 + (blk - 2) // 4
            return ci % 2, ((blk - 2) % 4) * 128

        def load_x(ci):
            tok0, n = chunk_geom(ci)
            b = ci % 2
            nt = n // 128
            for t in range(nt):
                P.dma("sp", lambda e, t=t: [e.dma_start(out=xb[b][:, t, :], in_=x_d[tok0 + t * 128:tok0 + (t + 1) * 128, :])],
                      ("xld", b, t), writes=[("xb", b, t)])
            P.dma("sp", lambda e: [e.dma_start(out=cs[b][:, :, 0:n], in_=cst_d[:, :, tok0:tok0 + n])],
                  ("csld", b), writes=[("cs", b)])

        def norm_a(ci, hi, t):
            b = ci % 2
            k = t % 2
            P.op("act", lambda e: e.activation(out=junk[:, k, :], in_=xb[b][:, t, :], func=AF.Square,
                                               accum_out=ssq[:, t:t + 1]),
                 reads=[("xb", b, t)], writes=[("ssq", t), ("junk", k)])
            P.op("act", lambda e: e.activation(out=rt4[:, t:t + 1], in_=ssq[:, t:t + 1], func=AF.Ln, scale=1.0 / D, bias=EPS),
                 reads=[("ssq", t)], writes=[("rt4", t)])
            P.op("act", lambda e: e.activation(out=rstd4[:, t:t + 1], in_=rt4[:, t:t + 1], func=AF.Exp, scale=-0.5),
                 reads=[("rt4", t)], writes=[("rstd4", t)])
            P.op("act", lambda e: e.activation(out=xs[k][:], in_=xb[b][:, t, :], func=AF.Copy, scale=rstd4[:, t:t + 1]),
                 reads=[("xb", b, t), ("rstd4", t)], writes=[("xs", k)])

        def norm_stats_all(ci):
            b = ci % 2
            for t in range(4):
                P.op("act", lambda e, t=t: e.activation(out=junk[:, t % 2, :], in_=xb[b][:, t, :], func=AF.Square,
                                                        accum_out=ssq[:, t:t + 1]),
                     reads=[("xb", b, t)], writes=[("ssq", t), ("junk", t % 2)])
            P.op("act", lambda e: e.activation(out=rt4[:, 0:4], in_=ssq[:, 0:4], func=AF.Ln, scale=1.0 / D, bias=EPS),
                 reads=[("ssq", t) for t in range(4)], writes=[("rt4", t) for t in range(4)])
            P.op("act", lambda e: e.activation(out=rstd4[:, 0:4], in_=rt4[:, 0:4], func=AF.Exp, scale=-0.5),
                 reads=[("rt4", t) for t in range(4)], writes=[("rstd4", t) for t in range(4)])

        def norm_scale(ci, t):
            b = ci % 2
            k = t % 2
            P.op("act", lambda e: e.activation(out=xs[k][:], in_=xb[b][:, t, :], func=AF.Copy, scale=rstd4[:, t:t + 1]),
                 reads=[("xb", b, t), ("rstd4", t)], writes=[("xs", k)])

        def norm_b(ci, hi, t, base=0, dve_only=False):
            k = t % 2
            kA, kB = base + k, base + k + 2
            psA = ps[kA][:].bitcast(BF16)
            psB = ps[kB][:].bitcast(BF16)

            def tr(e):
                r = None
                for c in range(8):
                    dst = psA if c < 4 else psB
                    r = e.transpose(out=dst[:, (c % 4) * 128:(c % 4 + 1) * 128], in_=xs[k][:, c * 128:(c + 1) * 128],
                                    identity=ident)
                return r
            P.op("pe", tr, reads=[("xs", k), "cmat_b"], writes=[("ps", kA), ("ps", kB)])
            P.op("dve", lambda e: e.tensor_copy(out=hT[hi][:, 0:4, t * 128:(t + 1) * 128],
                                                in_=psA[:, 0:512].rearrange("p (c m) -> p c m", c=4)),
                 reads=[("ps", kA)], writes=[("hT", hi, t)])
            if dve_only:
                P.op("dve", lambda e: e.tensor_copy(out=hT[hi][:, 4:8, t * 128:(t + 1) * 128],
                                                    in_=psB[:, 0:512].rearrange("p (c m) -> p c m", c=4)),
                     reads=[("ps", kB)], writes=[("hT", hi, t, "b")])
            else:
                P.op("act", lambda e: e.activation(out=hT[hi][:, 4:8, t * 128:(t + 1) * 128],
                                                   in_=psB[:, 0:512].rearrange("p (c m) -> p c m", c=4), func=AF.Copy),
                     reads=[("ps", kB)], writes=[("hT", hi, t, "b")])

        def norm_tile(ci, hi, t):
            norm_a(ci, hi, t)
            norm_b(ci, hi, t)

        def cgeo(ci):
            if ci == 0:
                return [1], 128, 128
            return [0, 1, 2, 3], 0, T

        def stage_a_in(ci, inter=None):
            tiles, col0, n = cgeo(ci)
            xc = 1 if ci == 0 else 0
            ncu = n + xc
            htk = [k_ for t_ in ([0, 1] if ci == 0 else tiles) for k_ in (("hT", 0 if ci % 2 == 0 else 2, t_),
                                                                        ("hT", 0 if ci % 2 == 0 else 2, t_, "b"))]
            hA = 0 if ci % 2 == 0 else 2
            for j in range(8):
                if inter is not None:
                    inter(j)
                slot = need(ci, PID_AIN[j])
                W = WR(slot).rearrange("p (a b) -> p a b", a=8)
                for gi, g in enumerate((1, 2, 3, 0)):
                    nn = ncu if gi < 2 else n
                    cc = col0 - xc if gi < 2 else col0

                    def mm(e, W=W, g=g, gi=gi, nn=nn, cc=cc):
                        r = None
                        for kc in range(8):
                            r = e.matmul(out=ps[gi][:, 0:nn], lhsT=W[:, kc, g * 128:(g + 1) * 128],
                                         rhs=hT[hA][:, kc, cc:cc + nn], start=(kc == 0), stop=(kc == 7))
                        return r
                    P.op("pe", mm, reads=[("wr", slot)] + htk, writes=[("ps", gi)])
                so = 6 * (j % 2)
                c_sb, v, sz, gg, a0, a1 = [sc[so + i] for i in range(6)]
                K = lambda i, so=so: ("sc", so + i)
                P.op("act", lambda e, c_sb=c_sb: e.activation(out=c_sb[:, 0:ncu], in_=ps[0][:, 0:ncu], func=AF.Copy),
                     reads=[("ps", 0)], writes=[K(0)])
                P.op("pool", lambda e, v=v, j=j: e.tensor_copy(out=v[:, 0:2], in_=vhalo[:, j, :]),
                     reads=[("vhalo", j)], writes=[("vh", so), ("sc", so + 1)])
                P.op("dve", lambda e, c_sb=c_sb, v=v: e.tensor_tensor(out=v[:, 2 - xc:2 + n], in0=c_sb[:, 0:ncu], in1=ps[1][:, 0:ncu],
                                                                      op=ALU.mult),
                     reads=[K(0), ("ps", 1), ("vh", so)], writes=[K(1), ("vh", so)])
                P.op("act", lambda e, sz=sz: e.activation(out=sz[:, 0:n], in_=ps[2][:, 0:n], func=AF.Silu),
                     reads=[("ps", 2)], writes=[K(2)])
                P.op("dve", lambda e, sz=sz, gg=gg: e.tensor_tensor(out=gg[:, 0:n], in0=sz[:, 0:n], in1=ps[3][:, 0:n],
                                                                    op=ALU.mult),
                     reads=[K(2), ("ps", 3)], writes=[K(3)])
                P.op("pool", lambda e, v=v, j=j: e.tensor_copy(out=vhalo[:, j, :], in_=v[:, n:n + 2]),
                     reads=[K(1), ("vh", so)], writes=[("vhalo", j)])
                P.op("pool", lambda e, v=v, a0=a0, j=j: e.tensor_scalar(out=a0[:, 0:n], in0=v[:, 0:n],
                                                                        scalar1=convw[:, 3 * j:3 * j + 1], scalar2=1.0,
                                                                        op0=ALU.mult, op1=ALU.mult),
                     reads=[K(1), ("vh", so), "convw"], writes=[K(4)])
                P.op("dve", lambda e, v=v, a0=a0, a1=a1, j=j: e.scalar_tensor_tensor(
                    out=a1[:, 0:n], in0=v[:, 1:1 + n], scalar=convw[:, 3 * j + 1:3 * j + 2], in1=a0[:, 0:n],
                    op0=ALU.mult, op1=ALU.add),
                    reads=[K(1), ("vh", so), K(4), "convw"], writes=[K(5)])
                P.op("dve", lambda e, v=v, a0=a0, a1=a1, j=j: e.scalar_tensor_tensor(
                    out=a0[:, 0:n], in0=v[:, 2:2 + n], scalar=convw[:, 3 * j + 2:3 * j + 3], in1=a1[:, 0:n],
                    op0=ALU.mult, op1=ALU.add),
                    reads=[K(1), K(5), "convw"], writes=[K(4)])
                P.op("pool", lambda e, a0=a0, gg=gg, j=j: e.tensor_tensor(out=yT[:, j, col0:col0 + n], in0=a0[:, 0:n], in1=gg[:, 0:n],
                                                                          op=ALU.mult),
                     reads=[K(4), K(3)], writes=[("yT", j, t_) for t_ in range(4)])

        def out_tile(ci, pids, t, banks, store=False):
            b = ci % 2
            slot0 = need(ci, pids[0])
            slot1 = (slot0 + 1) % 4
            Ws = [WR(slot0).rearrange("p (a b) -> p a b", a=4), WR(slot1).rearrange("p (a b) -> p a b", a=4)]
            for half in range(2):
                bank = banks[half]

                def mm(e, half=half, bank=bank):
                    r = None
                    for j in range(8):
                        r = e.matmul(out=ps[bank][:, :], lhsT=yT[:, j, t * 128:(t + 1) * 128],
                                     rhs=Ws[j // 4][:, j % 4, half * 512:(half + 1) * 512],
                                     start=(j == 0), stop=(j == 7))
                    return r
                P.op("pe", mm, reads=[("wr", slot0), ("wr", slot1)] + [("yT", j, t) for j in range(8)],
                     writes=[("ps", bank)])
                P.op("dve", lambda e, half=half, bank=bank: e.tensor_tensor(
                    out=xb[b][:, t, half * 512:(half + 1) * 512], in0=xb[b][:, t, half * 512:(half + 1) * 512],
                    in1=ps[bank][:, :], op=ALU.add),
                    reads=[("xb", b, t), ("ps", bank)], writes=[("xb", b, t)])
            if store:
                r0 = (ci - 1) * T + t * 128
                P.dma(STORE_Q, lambda e: [e.dma_start(out=out_d[r0:r0 + 128, :], in_=xb[b][:, t, :])],
                      ("ost", b, t), reads=[("xb", b, t)])

        def out_parts(ci, pids, t, banks, store=False):
            b = ci % 2
            slot0 = need(ci, pids[0])
            slot1 = (slot0 + 1) % 4
            Ws = [WR(slot0).rearrange("p (a b) -> p a b", a=4), WR(slot1).rearrange("p (a b) -> p a b", a=4)]
            parts = []
            for half in range(2):
                for jh in range(2):
                    def part(half=half, jh=jh):
                        bank = banks[half]

                        def mm(e):
                            r = None
                            for j in range(4 * jh, 4 * jh + 4):
                                r = e.matmul(out=ps[bank][:, :], lhsT=yT[:, j, t * 128:(t + 1) * 128],
                                             rhs=Ws[j // 4][:, j % 4, half * 512:(half + 1) * 512],
                                             start=(j == 0), stop=(j == 7))
                            return r
                        P.op("pe", mm, reads=[("wr", slot0 if jh == 0 else slot1)] + [("yT", j, t) for j in range(4 * jh, 4 * jh + 4)],
                             writes=[("ps", bank)])
                        if jh == 1:
                            P.op("dve", lambda e: e.tensor_tensor(
                                out=xb[b][:, t, half * 512:(half + 1) * 512], in0=xb[b][:, t, half * 512:(half + 1) * 512],
                                in1=ps[bank][:, :], op=ALU.add),
                                reads=[("xb", b, t), ("ps", bank)], writes=[("xb", b, t)])
                            if store and half == 1:
                                r0 = (ci - 1) * T + t * 128
                                P.dma(STORE_Q, lambda e: [e.dma_start(out=out_d[r0:r0 + 128, :], in_=xb[b][:, t, :])],
                                      ("ost", b, t), reads=[("xb", b, t)])
                    parts.append(part)
            return parts

        def out_pair(ci, pids, t0):
            b = ci % 2
            slot0 = need(ci, pids[0])
            slot1 = (slot0 + 1) % 4
            Ws = [WR(slot0).rearrange("p (a b) -> p a b", a=4), WR(slot1).rearrange("p (a b) -> p a b", a=4)]
            groups = [(t0 + dt, half, 4 + 2 * dt + half) for dt in range(2) for half in range(2)]

            def mm1(e):
                r = None
                for (t, half, bank) in groups:
                    for j in range(7):
                        r = e.matmul(out=ps[bank][:, :], lhsT=yT[:, j, t * 128:(t + 1) * 128],
                                     rhs=Ws[j // 4][:, j % 4, half * 512:(half + 1) * 512], start=(j == 0), stop=False)
                return r
            P.op("pe", mm1, reads=[("wr", slot0), ("wr", slot1)] + [("yT", j, t0 + dt) for j in range(7) for dt in range(2)],
                 writes=[("ps", bk) for (_, _, bk) in groups])
            for (t, half, bank) in groups:
                P.op("pe", lambda e, t=t, half=half, bank=bank: e.matmul(
                    out=ps[bank][:, :], lhsT=yT[:, 7, t * 128:(t + 1) * 128],
                    rhs=Ws[1][:, 3, half * 512:(half + 1) * 512], start=False, stop=True),
                    reads=[("wr", slot1), ("yT", 7, t)], writes=[("ps", bank)])
                P.op("dve", lambda e, t=t, half=half, bank=bank: e.tensor_tensor(
                    out=xb[b][:, t, half * 512:(half + 1) * 512], in0=xb[b][:, t, half * 512:(half + 1) * 512],
                    in1=ps[bank][:, :], op=ALU.add),
                    reads=[("xb", b, t), ("ps", bank)], writes=[("xb", b, t)])

        def chain_a(src_bank, n, cset):
            P.op("dve", lambda e: e.tensor_copy(out=qs[cset][:, 0, 0:n], in_=ps[src_bank][:, 0:n]),
                 reads=[("ps", src_bank)], writes=[("qs", cset, 0)])
            P.op("act", lambda e: e.activation(out=qs[cset][:, 1, 0:n], in_=ps[src_bank][:, 0:n], func=AF.Square),
                 reads=[("ps", src_bank)], writes=[("qs", cset, 1)])

        def chain_b(n, cset, ss_bank, rq_bank):
            P.op("pe", lambda e: e.matmul(out=ps[ss_bank][:, 0:n], lhsT=bdm, rhs=qs[cset][:, 1, 0:n], start=True, stop=True),
                 reads=[("qs", cset, 1), "cmat_b"], writes=[("ps", ss_bank)])
            P.op("pe", lambda e: e.matmul(out=ps[rq_bank][:, 0:n], lhsT=rotm, rhs=qs[cset][:, 0, 0:n], start=True, stop=True),
                 reads=[("qs", cset, 0), "cmat_b"], writes=[("ps", rq_bank)])

        def chain_c(src_bank, n, cset, ss_bank, rq_bank, gcol, b, dst_ap, dst_key, sset=0, col0=0):
            so = 4 * sset
            rt, rstd, tt, uu = [sc[so + i] for i in range(4)]
            K = lambda i: ("sc", so + i)
            P.op("act", lambda e: e.activation(out=rt[:, 0:n], in_=ps[ss_bank][:, 0:n], func=AF.Ln, scale=1.0 / 64, bias=EPS),
                 reads=[("ps", ss_bank)], writes=[K(0)])
            P.op("act", lambda e: e.activation(out=rstd[:, 0:n], in_=rt[:, 0:n], func=AF.Exp, scale=-0.5),
                 reads=[K(0)], writes=[K(1)])
            P.op("dve", lambda e: e.scalar_tensor_tensor(out=tt[:, 0:n], in0=ps[src_bank][:, 0:n], scalar=qkg[:, gcol:gcol + 1],
                                                         in1=cs[b][:, 0, col0:col0 + n], op0=ALU.mult, op1=ALU.mult),
                 reads=[("ps", src_bank), "qkg", ("cs", b)], writes=[K(2)])
            P.op("dve", lambda e: e.scalar_tensor_tensor(out=uu[:, 0:n], in0=ps[rq_bank][:, 0:n], scalar=qkg[:, gcol + 1:gcol + 2],
                                                         in1=cs[b][:, 1, col0:col0 + n], op0=ALU.mult, op1=ALU.mult),
                 reads=[("ps", rq_bank), "qkg", ("cs", b)], writes=[K(3)])
            P.op("pool", lambda e: e.tensor_tensor(out=tt[:, 0:n], in0=tt[:, 0:n], in1=uu[:, 0:n], op=ALU.add),
                 reads=[K(2), K(3)], writes=[K(2)])
            if isinstance(dst_ap, tuple):
                for hh, dap in enumerate(dst_ap):
                    P.op("pool", lambda e, hh=hh, dap=dap: e.tensor_tensor(out=dap, in0=tt[64 * hh:64 * hh + 64, 0:n],
                                                                         in1=rstd[64 * hh:64 * hh + 64, 0:n], op=ALU.mult),
                         reads=[K(2), K(1)], writes=dst_key if isinstance(dst_key, list) else [dst_key])
            else:
                P.op("pool", lambda e: e.tensor_tensor(out=dst_ap, in0=tt[:, 0:n], in1=rstd[:, 0:n], op=ALU.mult),
                     reads=[K(2), K(1)], writes=dst_key if isinstance(dst_key, list) else [dst_key])

        def stage_kvq(ci, pre_last=None):
            tok0, _n = chunk_geom(ci)
            tiles, col0, n = cgeo(ci)
            nt = len(tiles)
            b = ci % 2
            blk0 = tok0 // 128
            htk1 = [k_ for t_ in tiles for k_ in (("hT", 1, t_), ("hT", 1, t_, "b"))]
            slot = need(ci, PID_V)
            Wv = WR(slot)[:, 0:2048].rearrange("p (a b) -> p a b", a=8)
            for t in tiles:
                if t == tiles[-1] and pre_last is not None:
                    pre_last()
                bank = 4 + t
                c0 = 0

                def mm(e, t=t, bank=bank, c0=c0):
                    r = None
                    for kc in range(8):
                        r = e.matmul(out=ps[bank][:, c0:c0 + 256], lhsT=hT[1][:, kc, t * 128:(t + 1) * 128], rhs=Wv[:, kc, :],
                                     start=(kc == 0), stop=(kc == 7))
                    return r
                P.op("pe", mm, reads=[("wr", slot), ("hT", 1, t), ("hT", 1, t, "b")], writes=[("ps", bank)])
                vs = (blk0 + t) % 8
                src = ps[bank][:, c0:c0 + 256].rearrange("p (k d) -> p k d", k=4)
                P.op("act", lambda e, vs=vs, src=src: e.activation(out=Vp[vs][:, :, 0, 0:64], in_=src, func=AF.Copy),
                     reads=[("ps", bank)], writes=[("Vp", vs, 0)])
                P.op("dve", lambda e, vs=vs, src=src: e.tensor_copy(out=Vp[vs][:, :, 1, 64:128], in_=src),
                     reads=[("ps", bank)], writes=[("Vp", vs, 1)])
            if int(os.environ.get("K_SUB", "9")) <= 1:
                return
            ks = 0 if ci == 0 else ci % 2
            ulist = [("k", c) for c in range(2)]
            if ci > 0:
                ulist += [("q", c) for c in range(8)]
            slots = {}

            def pe_unit(u, kind, idx):
                if kind == "k":
                    if "k" not in slots:
                        slots["k"] = need(ci, PID_K)
                    slot = slots["k"]
                    Wk = WR(slot)[:, 2048:4096].rearrange("p (a b) -> p a b", a=8)
                    lhs = lambda kc: Wk[:, kc, idx * 128:(idx + 1) * 128]
                else:
                    h = idx // 4
                    if ("q", h) not in slots:
                        slots[("q", h)] = need(ci, PID_BQ[h])
                    slot = slots[("q", h)]
                    W = WR(slot).rearrange("p (a b) -> p a b", a=8)
                    lhs = lambda kc: W[:, kc, (idx % 4) * 128:(idx % 4 + 1) * 128]
                bank = u % 3

                def mm(e):
                    r = None
                    for kc in range(8):
                        r = e.matmul(out=ps[bank][:, 0:n], lhsT=lhs(kc), rhs=hT[1][:, kc, col0:col0 + n],
                                     start=(kc == 0), stop=(kc == 7))
                    return r
                P.op("pe", mm, reads=[("wr", slot)] + htk1, writes=[("ps", bank)])

            def fin_unit(u, kind, idx):
                cset = u % 2
                chain_b(n, cset, 3 + cset, 5 + cset)
                if kind == "k":
                    chain_c(u % 3, n, cset, 3 + cset, 5 + cset, 2, b,
                            (KTz[0:64, 0, 2 * idx, ks, col0:col0 + n], KTz[64:128, 1, 2 * idx + 1, ks, col0:col0 + n]),
                            [("KTn", ks, idx), ("KT", ks, 2 * idx), ("KT", ks, 2 * idx + 1)], sset=u % 3, col0=col0)
                    late[u + 3] = (lambda idx=idx: k_swap(idx))
                else:
                    chain_c(u % 3, n, cset, 3 + cset, 5 + cset, 0, b, QT[:, idx, 0:n], ("QT", idx), sset=u % 3)

            late = {}

            def k_swap(c):
                def mm(e):
                    e.matmul(out=ps[7][:, 0:n], lhsT=swapm, rhs=KTz[:, 0, 2 * c, ks, col0:col0 + n], start=True, stop=False)
                    return e.matmul(out=ps[7][:, 0:n], lhsT=swapm, rhs=KTz[:, 1, 2 * c + 1, ks, col0:col0 + n],
                                    start=False, stop=True)
                P.op("pe", mm, reads=[("KTn", ks, c), "cmat_b"], writes=[("ps", 7)])
                P.op("act", lambda e: e.activation(out=KTz[64:128, 1, 2 * c, ks, col0:col0 + n], in_=ps[7][64:128, 0:n],
                                                   func=AF.Copy),
                     reads=[("ps", 7)], writes=[("KT", ks, 2 * c)])
                P.op("act", lambda e: e.activation(out=KTz[0:64, 0, 2 * c + 1, ks, col0:col0 + n], in_=ps[7][0:64, 0:n],
                                                   func=AF.Copy),
                     reads=[("ps", 7)], writes=[("KT", ks, 2 * c + 1)])

            for u, (kind, idx) in enumerate(ulist):
                pe_unit(u, kind, idx)
                chain_a(u % 3, n, u % 2)
                if u >= 1:
                    fin_unit(u - 1, *ulist[u - 1])
                if u in late:
                    late.pop(u)()
            fin_unit(len(ulist) - 1, *ulist[-1])
            for u in sorted(late):
                late.pop(u)()
            if ci == 0:
                return
            zb = [7, 0, 1, 2]
            for c in range(8):
                h = c // 4
                slot = need(ci, PID_BZ[h])
                W = WR(slot).rearrange("p (a b) -> p a b", a=8)
                bank = zb[c % 4]

                def mm(e, W=W, c=c, bank=bank):
                    r = None
                    for kc in range(8):
                        r = e.matmul(out=ps[bank][:, 0:n], lhsT=W[:, kc, (c % 4) * 128:(c % 4 + 1) * 128],
                                     rhs=hT[1][:, kc, 0:n], start=(kc == 0), stop=(kc == 7))
                    return r
                P.op("pe", mm, reads=[("wr", slot)] + HT(1)[:2 * nt], writes=[("ps", bank)])
                sg = sc[4 * (c % 3)]
                P.op("act", lambda e, c=c, bank=bank, sg=sg: e.activation(out=sg[:, 0:n], in_=ps[bank][:, 0:n], func=AF.Sigmoid),
                     reads=[("ps", bank)], writes=[("sc", 4 * (c % 3))])
                P.op("dve", lambda e, c=c, bank=bank, sg=sg: e.tensor_tensor(out=szT[:, c, 0:n], in0=sg[:, 0:n], in1=ps[bank][:, 0:n],
                                                                          op=ALU.mult),
                     reads=[("ps", bank), ("sc", 4 * (c % 3))], writes=[("szT", c)])

        def stage_att(ci):
            tok0, n = chunk_geom(ci)
            nq = n // 128
            blk0 = tok0 // 128
            items = [(qb, kv) for qb in range(nq) for kv in range(4)]

            def S(i):
                qb, kv = items[i]
                blk = blk0 + qb
                st_ = i % 2
                kls = [kslot(blk - 1), kslot(blk)]

                def mm(e, srange=(0, 1)):
                    r = None
                    for bi in range(2):
                        ksl, kof = kls[bi]
                        for s in srange:
                            if F_S256:
                                r = e.matmul(out=ps[2 * st_ + s][:, bi * 256:(bi + 1) * 256],
                                             lhsT=KTz[:, s, kv, ksl, kof:kof + 128],
                                             rhs=QT[:, 2 * kv:2 * kv + 2, qb * 128:(qb + 1) * 128],
                                             start=True, stop=True)
                            else:
                                for gi in range(2):
                                    r = e.matmul(out=ps[2 * st_ + s][:, (bi * 2 + gi) * 128:(bi * 2 + gi + 1) * 128],
                                                 lhsT=KTz[:, s, kv, ksl, kof:kof + 128],
                                                 rhs=QT[:, 2 * kv + gi, qb * 128:(qb + 1) * 128],
                                                 start=True, stop=True)
                    return r
                reads = [("KT", kls[0][0], kv), ("KT", kls[1][0], kv), ("QT", 2 * kv), ("QT", 2 * kv + 1)]
                for s_ in range(2):
                    P.op("pe", lambda e, s_=s_: mm(e, (s_,)), reads=reads, writes=[("ps", 2 * st_ + s_)])

            def EM(i):
                qb, kv = items[i]
                blk = blk0 + qb
                st_ = i % 2
                mset = 0 if blk == 2 else 1
                mk = masks_b[:, 2 * mset:2 * mset + 2, :].unsqueeze(2).to_broadcast([128, 2, 2, 128])
                for s in range(2):
                    P.op("act", lambda e, s=s: e.activation(out=PT[st_][:, s, :], in_=ps[2 * st_ + s][:, :], func=AF.Exp,
                                                            scale=0.125),
                         reads=[("ps", 2 * st_ + s)], writes=[("PT", st_, s)])
                    pv = PT[st_][:, s, :].rearrange("p (a g q) -> p a g q", a=2, g=2)
                    P.op("dve" if F_MASKDVE else "pool", lambda e, pv=pv: e.tensor_tensor(out=pv, in0=pv, in1=mk, op=ALU.mult),
                         reads=[("PT", st_, s), "masks_b"], writes=[("PT", st_, s)])

            def PV(i):
                qb, kv = items[i]
                blk = blk0 + qb
                st_ = i % 2
                bank = 4 + st_
                vss = [(blk - 1) % 8, blk % 8]

                def mm(e):
                    r = None
                    k = 0
                    for s in range(2):
                        for bi in range(2):
                            r = e.matmul(out=ps[bank][:, 0:256], lhsT=Vp[vss[bi]][:, kv, s, :],
                                         rhs=PT[st_][:, s, bi * 256:(bi + 1) * 256], start=(k == 0), stop=(k == 3))
                            k += 1
                    r = e.matmul(out=ps[bank][:, 256:512], lhsT=sinkmat[:, kv * 128:(kv + 1) * 128], rhs=selmat[:, :],
                                 start=True, stop=False)
                    k = 0
                    for s in range(2):
                        for bi in range(2):
                            r = e.matmul(out=ps[bank][:, 256:512], lhsT=cmat_b[:, 3 + s, :],
                                         rhs=PT[st_][:, s, bi * 256:(bi + 1) * 256], start=False, stop=(k == 3))
                            k += 1
                    return r
                P.op("pe", mm, reads=[("PT", st_, 0), ("PT", st_, 1)] +
                     [("Vp", v_, h_) for v_ in vss for h_ in range(2)] +
                     ["sinkmat", "selmat", "cmat_b"], writes=[("ps", bank)])

            def NORM(i):
                qb, kv = items[i]
                st_ = i % 2
                bank = 4 + st_
                so = 6 * st_ + 4
                rD, tt = sc[so], sc[so + 1]
                P.op("act", lambda e: e.activation(out=rD[:, 0:256], in_=ps[bank][:, 256:512], func=AF.Ln),
                     reads=[("ps", bank)], writes=[("sc", so)])
                P.op("act", lambda e: e.activation(out=rD[:, 256:512], in_=rD[:, 0:256], func=AF.Exp, scale=-1.0),
                     reads=[("sc", so)], writes=[("sc", so)])
                P.op("dve", lambda e: e.tensor_tensor(out=tt[:, 0:256], in0=rD[:, 256:512], in1=ps[bank][:, 0:256], op=ALU.mult),
                     reads=[("ps", bank), ("sc", so)], writes=[("sc", so + 1)])
                P.op("pool", lambda e: e.tensor_tensor(out=yT[:, 2 * kv:2 * kv + 2, qb * 128:(qb + 1) * 128],
                                                       in0=tt[:, 0:256].rearrange("p (c q) -> p c q", c=2),
                                                       in1=szT[:, 2 * kv:2 * kv + 2, qb * 128:(qb + 1) * 128], op=ALU.mult),
                     reads=[("sc", so + 1), ("szT", 2 * kv), ("szT", 2 * kv + 1)], writes=[("yT", 2 * kv, qb), ("yT", 2 * kv + 1, qb)])

            pending = []

            def norm_and_out(j):
                NORM(j)
                if j % 4 == 3:
                    pending.extend(out_parts(ci, PID_BOUT, j // 4, (6, 7), store=True))

            ni = len(items)
            for i in range(ni):
                S(i)
                if pending:
                    pending.pop(0)()
                EM(i)
                if i >= 1:
                    PV(i - 1)
                if i >= 2:
                    norm_and_out(i - 2)
            PV(ni - 1)
            if ni >= 2:
                norm_and_out(ni - 2)
            norm_and_out(ni - 1)
            while pending:
                pending.pop(0)()

        def store_out(ci):
            b = ci % 2
            r0 = (ci - 1) * T
            P.dma("sp", lambda e: [e.dma_start(out=out_d[r0:r0 + T, :].rearrange("(t p) d -> p t d", p=128), in_=xb[b][:])],
                  ("ost", b), reads=XB(b))

        load_x(0)
        bg_load(0)
        for t in range(HALO // 128):
            norm_tile(0, 0, t)
        for ci in range(n_chunks + 1):
            tok0, n = chunk_geom(ci)
            nt = n // 128
            has_next = ci + 1 <= n_chunks
            if has_next:
                load_x(ci + 1)
            hN = 0 if (ci + 1) % 2 == 0 else 2

            def inter(j, ci=ci, hN=hN, has_next=has_next):
                if ci == 0 and j % 2 == 1:
                    bg_step(j // 2)
                if ci == 0:
                    j0 = 3
                    if has_next and j0 <= j <= j0 + 3:
                        norm_a(ci + 1, hN, j - j0)
                    if has_next and j0 + 1 <= j <= j0 + 4:
                        norm_b(ci + 1, hN, j - j0 - 1, base=4)
                elif has_next:
                    if j == 2:
                        norm_stats_all(ci + 1)
                        norm_scale(ci + 1, 0)
                        norm_scale(ci + 1, 1)
                    if j == 3:
                        norm_b(ci + 1, hN, 0, base=4)
                        norm_scale(ci + 1, 2)
                    if j == 4:
                        norm_b(ci + 1, hN, 1, base=4)
                        norm_scale(ci + 1, 3)
                    if j == 5:
                        norm_b(ci + 1, hN, 2, base=4)
                    if j == 6:
                        norm_b(ci + 1, hN, 3, base=4)
                if ci == 1:
                    bg_step(4 + j)
            stage_a_in(ci, inter)
            if ci == 0:
                out_tile(0, PID_AOUT, 1, (4, 5))
                norm_a(0, 1, 1)
                last_nb = (lambda: norm_b(0, 1, 1))
            else:
                out_pair(ci, PID_AOUT, 0)
                norm_a(ci, 1, 0)
                norm_a(ci, 1, 1)
                for t in range(2, nt):
                    out_tile(ci, PID_AOUT, t, (4 + 2 * (t % 2), 5 + 2 * (t % 2)))
                    norm_b(ci, 1, t - 2, dve_only=True)
                    norm_a(ci, 1, t)
                for t in range(max(nt - 2, 0), nt - 1):
                    norm_b(ci, 1, t, dve_only=True)
                last_nb = (lambda ci=ci, nt=nt: norm_b(ci, 1, nt - 1, dve_only=True))
            if dbg and ci == 1:
                P.dma("sp", lambda e: [e.dma_start(out=dbg_x1.rearrange("(t p) d -> p t d", p=128), in_=xb[1][:])],
                      "dbg", reads=XB(1))
            stage_kvq(ci, last_nb)
            if ci == 0:
                continue
            stage_att(ci)
            if dbg and ci == 1:
                P.dma("sp", lambda e: [e.dma_start(out=dbg_kt, in_=KTz[:, 0, :, :, :].rearrange("p a b c -> p (a b c)")),
                                       e.dma_start(out=dbg_qt, in_=QT[:].rearrange("p a b -> p (a b)")),
                                       e.dma_start(out=dbg_yt, in_=yT[:].rearrange("p a b -> p (a b)"))],
                      "dbg2", ndma=3, reads=[("KT", s_, k_) for s_ in range(2) for k_ in range(4)] +
                      [("QT", c) for c in range(8)] + [("yT", c, t_) for c in range(8) for t_ in range(4)])
        stats = P.emit()
        print("ops", len(P.ops), "signals/waits", stats)
    return nc


def _const_mats():
    ident = np.eye(128, dtype=np.float32)
    bd = np.zeros((128, 128), np.float32)
    bd[:64, :64] = 1.0
    bd[64:, 64:] = 1.0
    rot = np.zeros((128, 128), np.float32)
    for base in (0, 64):
        for i in range(8):
            rot[base + i + 8, base + i] = -1.0
            rot[base + i, base + i + 8] = 1.0
    elo = np.zeros((128, 128), np.float32)
    elo[:, :64] = 1.0
    ehi = np.zeros((128, 128), np.float32)
    ehi[:, 64:] = 1.0
    swap = np.zeros((128, 128), np.float32)
    for k in range(128):
        swap[k, (k + 64) % 128] = 1.0
    return np.ascontiguousarray(np.stack([ident, bd, rot, elo, ehi, swap], axis=1))


def _rope_table(pos):
    inv = (np.float32(ROPE_THETA) ** (-np.arange(0, 16, 2, dtype=np.float32) / np.float32(16))).astype(np.float32)
    ang = (pos.astype(np.float32)[:, None] * inv[None, :]).astype(np.float32)
    cos = np.cos(ang).astype(np.float32).T
    sin = np.sin(ang).astype(np.float32).T
    C = np.ones((64, len(pos)), np.float32)
    S = np.zeros((64, len(pos)), np.float32)
    C[0:8] = cos
    C[8:16] = cos
    S[0:8] = sin
    S[8:16] = sin
    C = np.concatenate([C, C], 0)
    S = np.concatenate([S, S], 0)
    return np.ascontiguousarray(np.stack([C, S], axis=1))


def _perm64(g):
    gp = g.copy()
    gp[0:8] = g[8:16]
    gp[8:16] = g[0:8]
    return gp


def make_in_maps(x, ln_a, w_in_a, conv_a, w_out_a, ln_kv, w_kv, k_norm, ln_b, w_in_b, q_norm, sinks, w_out_b,
                 starts, n_chunks):
    f = lambda a: np.ascontiguousarray(np.asarray(a, dtype=np.float32))
    x = f(x)
    gains = np.concatenate([f(ln_a)[0].reshape(8, 128).T, f(ln_kv).reshape(8, 128).T, f(ln_b)[0].reshape(8, 128).T], axis=1)
    convw = f(conv_a)[0, :, 0, :].reshape(3, 8, 128).transpose(2, 1, 0).reshape(128, 24)
    gq = f(q_norm)[0]
    gk = f(k_norm)
    qkg = np.stack([np.tile(gq, 2), np.tile(_perm64(gq), 2), np.tile(gk, 2), np.tile(_perm64(gk), 2)], axis=1)
    sinkrow = np.repeat(f(sinks)[0], 64).reshape(4, 2, 128).transpose(1, 0, 2).reshape(2, 512)
    sel = np.zeros((2, 256), np.float32)
    sel[0, 0:128] = 1.0
    sel[1, 128:256] = 1.0
    cmat = _const_mats()
    ci_ = np.arange(128)[:, None]
    ai_ = np.arange(128)[None, :]
    m_own = (ci_ <= ai_).astype(np.float32)
    m_prev = (ci_ > ai_).astype(np.float32)
    shared = dict(w_in_a=f(w_in_a)[0], w_out_a=f(w_out_a)[0], w_kv=f(w_kv), w_in_b=f(w_in_b)[0], w_out_b=f(w_out_b)[0],
                  gains=np.ascontiguousarray(gains), convw=np.ascontiguousarray(convw), qkg=np.ascontiguousarray(qkg),
                  sink2=np.ascontiguousarray(sinkrow), sel=sel, cmat=cmat)
    maps = []
    ntok = n_chunks * T
    for (bi, s0) in starts:
        xs = np.zeros((HALO + ntok, D), np.float32)
        if s0 >= HALO:
            xs[:] = x[bi, s0 - HALO:s0 + ntok]
            m_first = m_prev
        else:
            assert s0 == 0
            xs[HALO:] = x[bi, 0:ntok]
            m_first = np.zeros_like(m_prev)
        pos = np.maximum(np.arange(s0 - HALO, s0 + ntok), 0)
        masks = np.ascontiguousarray(np.stack([m_first, m_own, m_prev, m_own], axis=1))
        m = dict(shared)
        m.update(x=xs, cst=_rope_table(pos), masks=masks)
        maps.append(m)
    return maps


_NC_CACHE = {}


def kernel(x, ln_a, w_in_a, conv_a, w_out_a, ln_kv, w_kv, k_norm, ln_b, w_in_b, q_norm, sinks, w_out_b):
    n_chunks = TOK_CORE // T
    starts = [(c // 2, (c % 2) * TOK_CORE) for c in range(NCORE)]
    maps = make_in_maps(x, ln_a, w_in_a, conv_a, w_out_a, ln_kv, w_kv, k_norm, ln_b, w_in_b, q_norm, sinks, w_out_b,
                        starts, n_chunks)
    if "nc" not in _NC_CACHE:
        _NC_CACHE["nc"] = build_nc(n_chunks)
    res = run_bass_kernel_spmd(_NC_CACHE["nc"], maps, core_ids=list(range(NCORE)))
    out = np.empty((4, SEQ, D), np.float32)
    for c, (bi, s0) in enumerate(starts):
        out[bi, s0:s0 + TOK_CORE] = res.results[c]["out"]
    return out
```

```python
import contextlib
import numpy as np
import concourse.bass as bass
import concourse.mybir as mybir
from concourse.bass_utils import run_bass_kernel_spmd

F32 = mybir.dt.float32
BF16 = mybir.dt.bfloat16
AF = mybir.ActivationFunctionType
ALU = mybir.AluOpType

D = 1024
NCORE = 8
SEQ = 8192
TOK_CORE = 4096
HALO = 256
T = 512
EPS = 1e-6
ROPE_THETA = 500000.0
ENGS = ("pe", "act", "dve", "pool", "sp")
import os
FULL_DEPS = os.environ.get("K_FULL_DEPS", "1") == "1"
F_SINK2 = os.environ.get("K_SINK2", "1") == "1"
F_S256 = os.environ.get("K_S256", "1") == "1"
F_MASKDVE = os.environ.get("K_MASKDVE", "1") == "1"


class Prog:
    def __init__(self, nc):
        self.nc = nc
        self.ops = []
        self.eng_ops = {e: [] for e in ENGS}
        self.last_w = {}
        self.readers = {}
        self.dma_cnt = {}

    def _add(self, eng, fn, reads, writes, dma_key=None, ndma=1):
        idx = len(self.ops)
        deps = set()
        for r in reads:
            w = self.last_w.get(r)
            if w is not None:
                deps.add(w)
            if isinstance(r, tuple) and r[0] == "ps":
                for rd in self.readers.get(r, ()):
                    if self.ops[rd]["eng"] != eng:
                        deps.add(rd)
        for r in writes:
            w = self.last_w.get(r)
            if w is not None:
                deps.add(w)
            for rd in self.readers.get(r, ()):
                deps.add(rd)
        op = dict(idx=idx, eng=eng, fn=fn, dma_key=dma_key, ndma=ndma, signal=False)
        prev = self.eng_ops[eng][-1] if self.eng_ops[eng] else None
        know = dict(prev["know"]) if prev is not None else {}
        kept = []
        for d in sorted(deps, reverse=True):
            p = self.ops[d]
            src, seq = p["src"], p["seq"]
            if know.get(src, 0) >= seq:
                continue
            kept.append(d)
            for k2, v2 in p["know"].items():
                if know.get(k2, 0) < v2:
                    know[k2] = v2
            know[src] = seq
        op["deps"] = kept
        op["know"] = know
        if dma_key is not None:
            op["src"] = ("d", dma_key)
            op["seq"] = self.dma_cnt.get(dma_key, 0) + 16 * ndma
        else:
            op["src"] = ("e", eng)
            op["seq"] = len(self.eng_ops[eng]) + 1
        if dma_key is not None:
            self.dma_cnt[dma_key] = self.dma_cnt.get(dma_key, 0) + 16 * ndma
            op["dma_count"] = self.dma_cnt[dma_key]
        self.ops.append(op)
        self.eng_ops[eng].append(op)
        for r in reads:
            self.readers.setdefault(r, []).append(idx)
        for r in writes:
            self.last_w[r] = idx
            self.readers[r] = []
        return idx

    def op(self, eng, fn, reads=(), writes=()):
        return self._add(eng, fn, tuple(reads), tuple(writes))

    def dma(self, eng, fn, key, ndma=1, reads=(), writes=()):
        return self._add(eng, fn, tuple(reads), tuple(writes), dma_key=key, ndma=ndma)

    def emit(self):
        nc = self.nc
        ops = self.ops
        for op in ops:
            for d in op["deps"]:
                if ops[d]["dma_key"] is None:
                    ops[d]["signal"] = True
        cnt = {e: 0 for e in ENGS}
        for e in ENGS:
            for op in self.eng_ops[e]:
                if op["dma_key"] is None and op["signal"]:
                    cnt[e] += 1
                    op["count"] = cnt[e]
        dma_keys = sorted(self.dma_cnt.keys(), key=str)
        nwait = {e: 0 for e in ENGS}
        with contextlib.ExitStack() as st:
            esem = {e: st.enter_context(nc.semaphore("s_" + e)) for e in ENGS}
            dsem = {k: st.enter_context(nc.semaphore("d_%d" % i)) for i, k in enumerate(dma_keys)}
            block = st.enter_context(nc.Block())

            def make(e):
                def body(eng):
                    waited = {}
                    for op in self.eng_ops[e]:
                        for d in op["deps"]:
                            p = ops[d]
                            if p["dma_key"] is not None:
                                key = ("d", p["dma_key"])
                                val = p["dma_count"]
                                sem = dsem[p["dma_key"]]
                            else:
                                key = ("e", p["eng"])
                                val = p["count"]
                                sem = esem[p["eng"]]
                            if waited.get(key, 0) >= val:
                                continue
                            waited[key] = val
                            eng.wait_ge(sem, val)
                            nwait[e] += 1
                        r = op["fn"](eng)
                        if op["dma_key"] is not None:
                            insts = r if isinstance(r, (list, tuple)) else [r]
                            assert len(insts) == op["ndma"], (len(insts), op["ndma"])
                            for i in insts:
                                i.then_inc(dsem[op["dma_key"]], 16)
                        elif op["signal"]:
                            r.then_inc(esem[e], 1)
                    if e == "sp":
                        for k in dma_keys:
                            eng.wait_ge(dsem[k], self.dma_cnt[k])
                return body

            block.tensor(make("pe"))
            block.scalar(make("act"))
            block.vector(make("dve"))
            block.gpsimd(make("pool"))
            block.sync(make("sp"))
        return cnt, nwait


PID_AIN = list(range(0, 8))
PID_AOUT = [8, 9]
PID_V = 10
PID_K = 10
PID_BQ = [11, 12]
PID_BZ = [13, 14]
PID_BOUT = [15, 16]
NPIECE = 17


def build_nc(n_chunks=8, dbg=False):
    nc = bass.Bass("TRN2", target_bir_lowering=False)
    TT = HALO + n_chunks * T

    def din(name, shape, dt=F32):
        return nc.dram_tensor(name, shape, dt, kind="ExternalInput").ap()

    x_d = din("x", [TT, D])
    w_in_a = din("w_in_a", [D, 4096])
    w_out_a = din("w_out_a", [D, D])
    w_kv = din("w_kv", [D, 512])
    w_in_b = din("w_in_b", [D, 2048])
    w_out_b = din("w_out_b", [D, D])
    gains_d = din("gains", [128, 24])
    convw_d = din("convw", [128, 24])
    qkg_d = din("qkg", [128, 4])
    sink_d = din("sink2", [2, 4 * 128])
    cst_d = din("cst", [128, 2, TT])
    masks_d = din("masks", [128, 4, 128])
    cmat_d = din("cmat", [128, 6, 128])
    sel_d = din("sel", [2, 256])
    out_d = nc.dram_tensor("out", [n_chunks * T, D], F32, kind="ExternalOutput").ap()
    wscr = nc.dram_tensor("wscr", [NPIECE, 128, 4096], BF16, kind="Internal").ap()
    if dbg:
        dbg_x1 = nc.dram_tensor("dbg_x1", [T, D], F32, kind="ExternalOutput").ap()
        dbg_kt = nc.dram_tensor("dbg_kt", [128, 4 * 2 * T], BF16, kind="ExternalOutput").ap()
        dbg_qt = nc.dram_tensor("dbg_qt", [128, 8 * T], BF16, kind="ExternalOutput").ap()
        dbg_yt = nc.dram_tensor("dbg_yt", [128, 8 * T], BF16, kind="ExternalOutput").ap()

    with contextlib.ExitStack() as st:
        def sb(name, shape, dt):
            return st.enter_context(nc.sbuf_tensor(name, shape, dt))

        xb = [sb("xb%d" % i, [128, 4, D], F32) for i in range(2)]
        wrt = sb("wr", [128, 4, 4096], BF16)
        WR = lambda slot: wrt[:, slot, :]
        junk = sb("junk", [128, 2, D], BF16)
        xs = [sb("xs%d" % i, [128, D], BF16) for i in range(2)]
        hty = sb("hty", [128, 4, 8 * T], BF16)
        hT = [hty[:, i, :].rearrange("p (a b) -> p a b", a=8) for i in range(3)]
        NSC = 12
        sc = [sb("sc%d" % i, [128, 520], F32) for i in range(NSC)]
        qs = [sb("qs%d" % i, [128, 2, T], BF16) for i in range(2)]
        yT = hty[:, 3, :].rearrange("p (a b) -> p a b", a=8)
        KTz = sb("KTz", [128, 2, 4, 2, T], BF16)
        Vp = [sb("Vp%d" % i, [128, 4, 2, 128], BF16) for i in range(8)]
        QT = sb("QT", [128, 8, T], BF16)
        cs = [sb("cs%d" % i, [128, 2, T], F32) for i in range(2)]
        szT = sb("szT", [128, 8, T], BF16)
        PT = [sb("PT%d" % i, [128, 2, 512], BF16) for i in range(2)]
        ssq = sb("ssq", [128, 4], F32)
        rt4 = sb("rt4", [128, 4], F32)
        rstd4 = sb("rstd4", [128, 4], F32)
        gains = sb("gains_s", [128, 24], F32)
        convw = sb("convw_s", [128, 24], F32)
        qkg = sb("qkg_s", [128, 4], F32)
        vhalo = sb("vhalo", [128, 8, 2], F32)
        cmat_f = sb("cmat_f", [128, 6, 128], F32)
        cmat_b = sb("cmat_b", [128, 6, 128], BF16)
        masks_f = sb("masks_f", [128, 4, 128], F32)
        masks_b = sb("masks_b", [128, 4, 128], BF16)
        sink_f = sb("sink_f", [2, 512], F32)
        sink_b = sb("sink_b", [2, 512], BF16)
        sel_f = sb("sel_f", [2, 256], F32)
        sel_b = sb("sel_b", [2, 256], BF16)
        sinkmat = sb("sinkmat", [128, 512], BF16)
        selmat = sb("selmat", [128, 256], BF16)
        ps = [st.enter_context(nc.psum_tensor("ps%d" % i, [128, 512], F32)) for i in range(8)]

        ident = cmat_b[:, 0, :]
        bdm = cmat_b[:, 1, :]
        rotm = cmat_b[:, 2, :]
        swapm = cmat_b[:, 5, :]

        P = Prog(nc)
        XB = lambda b: [("xb", b, t) for t in range(4)]
        HT = lambda h: [k_ for t in range(4) for k_ in (("hT", h, t), ("hT", h, t, "b"))]

        P.dma("sp", lambda e: [
            e.dma_start(out=gains[:], in_=gains_d), e.dma_start(out=convw[:], in_=convw_d),
            e.dma_start(out=qkg[:], in_=qkg_d), e.dma_start(out=sink_f[:], in_=sink_d),
            e.dma_start(out=masks_f[:], in_=masks_d), e.dma_start(out=cmat_f[:], in_=cmat_d),
            e.dma_start(out=sel_f[:], in_=sel_d)],
            "const", ndma=7, writes=["gains", "convw", "qkg", "sink_f", "masks_f", "cmat_f", "sel_f"])
        P.op("dve", lambda e: e.tensor_copy(out=cmat_b[:], in_=cmat_f[:]), reads=["cmat_f"], writes=["cmat_b"])
        P.op("dve", lambda e: e.tensor_copy(out=masks_b[:], in_=masks_f[:]), reads=["masks_f"], writes=["masks_b"])
        P.op("dve", lambda e: e.tensor_copy(out=sel_b[:], in_=sel_f[:]), reads=["sel_f"], writes=["sel_b"])
        P.op("act", lambda e: e.activation(out=sink_b[:], in_=sink_f[:], func=AF.Exp), reads=["sink_f"], writes=["sink_b"])
        P.op("pool", lambda e: e.memset(sinkmat[:], 0.0), writes=["sinkmat"])
        P.op("pool", lambda e: e.memset(selmat[:], 0.0), writes=["selmat"])
        P.op("dve", lambda e: e.tensor_copy(out=sinkmat[0:2, :], in_=sink_b[:]), reads=["sink_b", "sinkmat"], writes=["sinkmat"])
        P.op("dve", lambda e: e.tensor_copy(out=selmat[0:2, :], in_=sel_b[:]), reads=["sel_b", "selmat"], writes=["selmat"])
        P.op("pool", lambda e: e.memset(KTz[:], 0.0), writes=[("KT", s_, k_) for s_ in range(2) for k_ in range(4)])
        P.op("pool", lambda e: e.memset(vhalo[:], 0.0), writes=[("vhalo", j) for j in range(8)])
        for i in range(8):
            P.op("pool", (lambda i: lambda e: e.memset(Vp[i][:], 0.0))(i), writes=[("Vp", i)])

        rr = [0]

        def conv_op(out_ap, in_ap, gain_ap, reads, writes):
            engs = ("act", "dve", "pool")
            eng = engs[rr[0] % len(engs)]
            rr[0] += 1
            if eng == "act":
                if gain_ap is None:
                    fn = lambda e: e.activation(out=out_ap, in_=in_ap, func=AF.Copy)
                else:
                    fn = lambda e: e.activation(out=out_ap, in_=in_ap, func=AF.Copy, scale=gain_ap)
            else:
                if gain_ap is None:
                    fn = lambda e: e.tensor_copy(out=out_ap, in_=in_ap)
                else:
                    fn = lambda e: e.tensor_scalar(out=out_ap, in0=in_ap, scalar1=gain_ap, scalar2=1.0,
                                                   op0=ALU.mult, op1=ALU.mult)
            P.op(eng, fn, reads=reads, writes=writes)

        wia = w_in_a.rearrange("(kc p) f -> p kc f", p=128)
        woa = w_out_a.rearrange("(kc p) f -> p kc f", p=128)
        wkv = w_kv.rearrange("(kc p) f -> p kc f", p=128)
        wib = w_in_b.rearrange("(kc p) f -> p kc f", p=128)
        wob = w_out_b.rearrange("(kc p) f -> p kc f", p=128)

        units = []

        ALTK = [k_ for h_ in range(3) for k_ in HT(h_)] + [("yT", j_, t_) for j_ in range(8) for t_ in range(4)]

        def store_piece(pid, slot):
            if slot < 4:
                P.dma("sp", lambda e: [e.dma_start(out=wscr[pid], in_=WR(slot))],
                      ("wst", slot), reads=[("wr", slot)], writes=[("wscr", pid)])
            else:
                P.dma("sp", lambda e: [e.dma_start(out=wscr[pid], in_=hty[:, slot - 4, :])],
                      ("wst", slot), reads=ALTK, writes=[("wscr", pid)])

        for jq in range(2):
            for g in range(4):
                for kh in range(2):
                    def ld(stg, jq=jq, g=g, kh=kh):
                        stg3 = stg.rearrange("p (a b) -> p a b", a=4)
                        c0 = g * 1024 + jq * 512
                        return lambda e: [e.dma_start(out=stg3, in_=wia[:, 4 * kh:4 * kh + 4, c0:c0 + 512])]

                    def cv(stg, sk, jq=jq, g=g, kh=kh):
                        stg3 = stg.rearrange("p (a b) -> p a b", a=4)
                        dst_t = wrt if jq == 0 else hty
                        wkeys = [("wr", q) for q in range(4)] if jq == 0 else ALTK
                        for k4 in range(4):
                            kc = 4 * kh + k4
                            out_ap = dst_t[:, :, kc * 512 + g * 128: kc * 512 + (g + 1) * 128]
                            in_ap = stg3[:, k4, :].rearrange("p (j m) -> p j m", j=4)
                            conv_op(out_ap, in_ap, gains[:, kc:kc + 1], sk + ["gains"], wkeys)
                    stores = [(4 * jq + jj, jj + 4 * jq) for jj in range(4)] if (g == 3 and kh == 1) else []
                    units.append((ld, cv, stores))
        grp_out = {}
        for pid in PID_AOUT + PID_BOUT:
            src = woa if pid in PID_AOUT else wob
            h = pid - (PID_AOUT[0] if pid in PID_AOUT else PID_BOUT[0])
            slot = pid % 4
            lst = []
            for hh in range(2):
                def ld(stg, src=src, h=h, hh=hh):
                    stg3 = stg.rearrange("p (a b) -> p a b", a=2)
                    return lambda e: [e.dma_start(out=stg3, in_=src[:, 4 * h + 2 * hh:4 * h + 2 * hh + 2, :])]

                def cv(stg, sk, slot=slot, hh=hh):
                    stg3 = stg.rearrange("p (a b) -> p a b", a=2)
                    dst3 = WR(slot).rearrange("p (a b) -> p a b", a=4)
                    for a in range(2):
                        conv_op(dst3[:, 2 * hh + a, :], stg3[:, a, :], None, sk, [("wr", slot)])
                lst.append((ld, cv, [(pid, slot)] if hh == 1 else []))
            grp_out[pid] = lst
        kv_units = []
        for kh in range(2):
            def ld_kv(stg, kh=kh):
                stg3 = stg.rearrange("p (a b) -> p a b", a=4)
                return lambda e: [e.dma_start(out=stg3, in_=wkv[:, 4 * kh:4 * kh + 4, :])]

            def cv_kv(stg, sk, kh=kh):
                stg3 = stg.rearrange("p (a b) -> p a b", a=4)
                sv = PID_V % 4
                dstv = WR(sv)[:, 0:2048].rearrange("p (a b) -> p a b", a=8)
                dstk = WR(sv)[:, 2048:4096].rearrange("p (a b) -> p a b", a=8)
                for k4 in range(4):
                    kc = 4 * kh + k4
                    conv_op(dstv[:, kc, :], stg3[:, k4, 256:512], gains[:, 8 + kc:9 + kc], sk + ["gains"], [("wr", sv)])
                    conv_op(dstk[:, kc, :], stg3[:, k4, 0:256], gains[:, 8 + kc:9 + kc], sk + ["gains"], [("wr", sv)])
            kv_units.append((ld_kv, cv_kv, [(PID_V, PID_V % 4)] if kh == 1 else []))
        grp_b = {}
        for pid in PID_BQ + PID_BZ:
            c0 = (pid - PID_BQ[0]) * 512 if pid in PID_BQ else 1024 + (pid - PID_BZ[0]) * 512
            slot = pid % 4
            lst = []
            for kh in range(2):
                def ld(stg, c0=c0, kh=kh):
                    stg3 = stg.rearrange("p (a b) -> p a b", a=4)
                    return lambda e: [e.dma_start(out=stg3, in_=wib[:, 4 * kh:4 * kh + 4, c0:c0 + 512])]

                def cv(stg, sk, slot=slot, kh=kh):
                    stg3 = stg.rearrange("p (a b) -> p a b", a=4)
                    dst3 = WR(slot).rearrange("p (a b) -> p a b", a=8)
                    for k4 in range(4):
                        kc = 4 * kh + k4
                        conv_op(dst3[:, kc, :], stg3[:, k4, :], gains[:, 16 + kc:17 + kc], sk + ["gains"], [("wr", slot)])
                lst.append((ld, cv, [(pid, slot)] if kh == 1 else []))
            grp_b[pid] = lst
        for pid in PID_AOUT:
            units += grp_out[pid]
        units += kv_units

        bgu = []
        ptflat = [PT[a][:].rearrange("p a b -> p (a b)") for a in range(2)]
        for pid in PID_BQ + PID_BZ:
            for kh in range(2):
                bgu.append(("in", None, pid, None, kh))
        for pid in PID_BOUT:
            for hh in range(2):
                bgu.append(("out", wob, pid, pid - PID_BOUT[0], hh))
        bg_stage = [QT[:].rearrange("p a b -> p (a b)").bitcast(F32), szT[:].rearrange("p a b -> p (a b)").bitcast(F32)]
        bg_skeys = [[("QT", c) for c in range(8)], [("szT", c) for c in range(8)]]
        PTK = lambda a: [("PT", a, 0), ("PT", a, 1)]

        def bg_load(k):
            kind, src, pid, h, hh = bgu[k]
            stg = bg_stage[k % 2]
            if kind == "out":
                stg3 = stg.rearrange("p (a b) -> p a b", a=2)
                fn = lambda e: [e.dma_start(out=stg3, in_=src[:, 4 * h + 2 * hh:4 * h + 2 * hh + 2, :])]
            else:
                c0 = (pid - PID_BQ[0]) * 512 if pid in PID_BQ else 1024 + (pid - PID_BZ[0]) * 512
                stg3 = stg.rearrange("p (a b) -> p a b", a=4)
                fn = lambda e: [e.dma_start(out=stg3, in_=wib[:, 4 * hh:4 * hh + 4, c0:c0 + 512])]
            P.dma("sp", fn, ("bgl", k % 2), writes=bg_skeys[k % 2])

        def bg_step(k):
            if k >= len(bgu):
                return
            if k + 1 < len(bgu):
                bg_load(k + 1)
            kind, src, pid, h, hh = bgu[k]
            stg = bg_stage[k % 2]
            sk = bg_skeys[k % 2]
            if kind == "out":
                stg3 = stg.rearrange("p (a b) -> p a b", a=2)
                for a in range(2):
                    for half in range(2):
                        conv_op(PT[a][:, half, :], stg3[:, a, half * 512:(half + 1) * 512], None, sk, PTK(a))
                offs = [(2 * hh + a) * 1024 for a in range(2)]
            else:
                stg3 = stg.rearrange("p (a b) -> p a b", a=4)
                for k4 in range(4):
                    kc = 4 * hh + k4
                    conv_op(PT[k4 // 2][:, k4 % 2, :], stg3[:, k4, :], gains[:, 16 + kc:17 + kc], sk + ["gains"], PTK(k4 // 2))
                offs = [(4 * hh + 2 * a) * 512 for a in range(2)]
            P.dma("sp", lambda e: [e.dma_start(out=wscr[pid][:, offs[a]:offs[a] + 1024], in_=ptflat[a]) for a in range(2)],
                  ("bgs",), ndma=2, reads=PTK(0) + PTK(1), writes=[("wscr", pid)])

        NSTG = 6

        def stg_of(k):
            q = k % NSTG
            if q < 4:
                return xb[q // 2][:, 2 * (q % 2):2 * (q % 2) + 2, :].rearrange("p a b -> p (a b)")
            return bg_stage[q - 4]

        def stg_keys(k):
            q = k % NSTG
            if q < 4:
                return [("xb", q // 2, 2 * (q % 2)), ("xb", q // 2, 2 * (q % 2) + 1)]
            return bg_skeys[q - 4]

        def rec_load(k):
            P.dma("sp", units[k][0](stg_of(k)), ("stg", k % NSTG), writes=stg_keys(k))

        AHEAD = 5
        for k in range(min(AHEAD, len(units))):
            rec_load(k)
        for k in range(len(units)):
            if k + AHEAD < len(units):
                rec_load(k + AHEAD)
            units[k][1](stg_of(k), stg_keys(k))
            for (pid, slot) in units[k][2]:
                store_piece(pid, slot)

        uses = []
        for ci in range(n_chunks + 1):
            npc = 11 if ci == 0 else NPIECE
            for pid in range(npc):
                uses.append((ci, pid))
        use_index = {u: k for k, u in enumerate(uses)}
        nl = [0]

        def need(ci, pid):
            k = use_index[(ci, pid)]
            while nl[0] <= min(k + 3, len(uses) - 1):
                kk = nl[0]
                p2 = uses[kk][1]
                slot = kk % 4
                if kk != 3:
                    P.dma("sp", (lambda p2, slot: lambda e: [e.dma_start(out=WR(slot), in_=wscr[p2])])(p2, slot),
                          ("wld", slot), reads=[("wscr", p2)], writes=[("wr", slot)])
                nl[0] += 1
            return k % 4

        def chunk_geom(ci):
            if ci == 0:
                return 0, HALO
            return HALO + (ci - 1) * T, T

        def kslot(blk):
            if blk < 2:
                return 0, blk * 128
            ci = 1 + (blk - 2) // 4
            return ci % 2, ((blk - 2) % 4) * 128

        def load_x(ci):
            tok0, n = chunk_geom(ci)
            b = ci % 2
            nt = n // 128
            for t in range(nt):
                P.dma("sp", lambda e, t=t: [e.dma_start(out=xb[b][:, t, :], in_=x_d[tok0 + t * 128:tok0 + (t + 1) * 128, :])],
                      ("xld", b, t), writes=[("xb", b, t)])
            P.dma("sp", lambda e: [e.dma_start(out=cs[b][:, :, 0:n], in_=cst_d[:, :, tok0:tok0 + n])],
                  ("csld", b), writes=[("cs", b)])

        def norm_a(ci, hi, t):
            b = ci % 2
            k = t % 2
            P.op("act", lambda e: e.activation(out=junk[:, k, :], in_=xb[b][:, t, :], func=AF.Square,
                                               accum_out=ssq[:, t:t + 1]),
                 reads=[("xb", b, t)], writes=[("ssq", t), ("junk", k)])
            P.op("act", lambda e: e.activation(out=rt4[:, t:t + 1], in_=ssq[:, t:t + 1], func=AF.Ln, scale=1.0 / D, bias=EPS),
                 reads=[("ssq", t)], writes=[("rt4", t)])
            P.op("act", lambda e: e.activation(out=rstd4[:, t:t + 1], in_=rt4[:, t:t + 1], func=AF.Exp, scale=-0.5),
                 reads=[("rt4", t)], writes=[("rstd4", t)])
            P.op("act", lambda e: e.activation(out=xs[k][:], in_=xb[b][:, t, :], func=AF.Copy, scale=rstd4[:, t:t + 1]),
                 reads=[("xb", b, t), ("rstd4", t)], writes=[("xs", k)])

        def norm_stats_all(ci):
            b = ci % 2
            for t in range(4):
                P.op("act", lambda e, t=t: e.activation(out=junk[:, t % 2, :], in_=xb[b][:, t, :], func=AF.Square,
                                                        accum_out=ssq[:, t:t + 1]),
                     reads=[("xb", b, t)], writes=[("ssq", t), ("junk", t % 2)])
            P.op("act", lambda e: e.activation(out=rt4[:, 0:4], in_=ssq[:, 0:4], func=AF.Ln, scale=1.0 / D, bias=EPS),
                 reads=[("ssq", t) for t in range(4)], writes=[("rt4", t) for t in range(4)])
            P.op("act", lambda e: e.activation(out=rstd4[:, 0:4], in_=rt4[:, 0:4], func=AF.Exp, scale=-0.5),
                 reads=[("rt4", t) for t in range(4)], writes=[("rstd4", t) for t in range(4)])

        def norm_scale(ci, t):
            b = ci % 2
            k = t % 2
            P.op("act", lambda e: e.activation(out=xs[k][:], in_=xb[b][:, t, :], func=AF.Copy, scale=rstd4[:, t:t + 1]),
                 reads=[("xb", b, t), ("rstd4", t)], writes=[("xs", k)])

        def norm_b(ci, hi, t, base=0, dve_only=False):
            k = t % 2
            kA, kB = base + k, base + k + 2
            psA = ps[kA][:].bitcast(BF16)
            psB = ps[kB][:].bitcast(BF16)

            def tr(e):
                r = None
                for c in range(8):
                    dst = psA if c < 4 else psB
                    r = e.transpose(out=dst[:, (c % 4) * 128:(c % 4 + 1) * 128], in_=xs[k][:, c * 128:(c + 1) * 128],
                                    identity=ident)
                return r
            P.op("pe", tr, reads=[("xs", k), "cmat_b"], writes=[("ps", kA), ("ps", kB)])
            P.op("dve", lambda e: e.tensor_copy(out=hT[hi][:, 0:4, t * 128:(t + 1) * 128],
                                                in_=psA[:, 0:512].rearrange("p (c m) -> p c m", c=4)),
                 reads=[("ps", kA)], writes=[("hT", hi, t)])
            if dve_only:
                P.op("dve", lambda e: e.tensor_copy(out=hT[hi][:, 4:8, t * 128:(t + 1) * 128],
                                                    in_=psB[:, 0:512].rearrange("p (c m) -> p c m", c=4)),
                     reads=[("ps", kB)], writes=[("hT", hi, t, "b")])
            else:
                P.op("act", lambda e: e.activation(out=hT[hi][:, 4:8, t * 128:(t + 1) * 128],
                                                   in_=psB[:, 0:512].rearrange("p (c m) -> p c m", c=4), func=AF.Copy),
                     reads=[("ps", kB)], writes=[("hT", hi, t, "b")])

        def norm_tile(ci, hi, t):
            norm_a(ci, hi, t)
            norm_b(ci, hi, t)

        def cgeo(ci):
            if ci == 0:
                return [1], 128, 128
            return [0, 1, 2, 3], 0, T

        def stage_a_in(ci, inter=None):
            tiles, col0, n = cgeo(ci)
            xc = 1 if ci == 0 else 0
            ncu = n + xc
            htk = [k_ for t_ in ([0, 1] if ci == 0 else tiles) for k_ in (("hT", 0 if ci % 2 == 0 else 2, t_),
                                                                        ("hT", 0 if ci % 2 == 0 else 2, t_, "b"))]
            hA = 0 if ci % 2 == 0 else 2
            for j in range(8):
                if inter is not None:
                    inter(j)
                slot = need(ci, PID_AIN[j])
                W = WR(slot).rearrange("p (a b) -> p a b", a=8)
                for gi, g in enumerate((1, 2, 3, 0)):
                    nn = ncu if gi < 2 else n
                    cc = col0 - xc if gi < 2 else col0

                    def mm(e, W=W, g=g, gi=gi, nn=nn, cc=cc):
                        r = None
                        for kc in range(8):
                            r = e.matmul(out=ps[gi][:, 0:nn], lhsT=W[:, kc, g * 128:(g + 1) * 128],
                                         rhs=hT[hA][:, kc, cc:cc + nn], start=(kc == 0), stop=(kc == 7))
                        return r
                    P.op("pe", mm, reads=[("wr", slot)] + htk, writes=[("ps", gi)])
                so = 6 * (j % 2)
                c_sb, v, sz, gg, a0, a1 = [sc[so + i] for i in range(6)]
                K = lambda i, so=so: ("sc", so + i)
                P.op("act", lambda e, c_sb=c_sb: e.activation(out=c_sb[:, 0:ncu], in_=ps[0][:, 0:ncu], func=AF.Copy),
                     reads=[("ps", 0)], writes=[K(0)])
                P.op("pool", lambda e, v=v, j=j: e.tensor_copy(out=v[:, 0:2], in_=vhalo[:, j, :]),
                     reads=[("vhalo", j)], writes=[("vh", so), ("sc", so + 1)])
                P.op("dve", lambda e, c_sb=c_sb, v=v: e.tensor_tensor(out=v[:, 2 - xc:2 + n], in0=c_sb[:, 0:ncu], in1=ps[1][:, 0:ncu],
                                                                      op=ALU.mult),
                     reads=[K(0), ("ps", 1), ("vh", so)], writes=[K(1), ("vh", so)])
                P.op("act", lambda e, sz=sz: e.activation(out=sz[:, 0:n], in_=ps[2][:, 0:n], func=AF.Silu),
                     reads=[("ps", 2)], writes=[K(2)])
                P.op("dve", lambda e, sz=sz, gg=gg: e.tensor_tensor(out=gg[:, 0:n], in0=sz[:, 0:n], in1=ps[3][:, 0:n],
                                                                    op=ALU.mult),
                     reads=[K(2), ("ps", 3)], writes=[K(3)])
                P.op("pool", lambda e, v=v, j=j: e.tensor_copy(out=vhalo[:, j, :], in_=v[:, n:n + 2]),
                     reads=[K(1), ("vh", so)], writes=[("vhalo", j)])
                P.op("pool", lambda e, v=v, a0=a0, j=j: e.tensor_scalar(out=a0[:, 0:n], in0=v[:, 0:n],
                                                                        scalar1=convw[:, 3 * j:3 * j + 1], scalar2=1.0,
                                                                        op0=ALU.mult, op1=ALU.mult),
                     reads=[K(1), ("vh", so), "convw"], writes=[K(4)])
                P.op("dve", lambda e, v=v, a0=a0, a1=a1, j=j: e.scalar_tensor_tensor(
                    out=a1[:, 0:n], in0=v[:, 1:1 + n], scalar=convw[:, 3 * j + 1:3 * j + 2], in1=a0[:, 0:n],
                    op0=ALU.mult, op1=ALU.add),
                    reads=[K(1), ("vh", so), K(4), "convw"], writes=[K(5)])
                P.op("dve", lambda e, v=v, a0=a0, a1=a1, j=j: e.scalar_tensor_tensor(
                    out=a0[:, 0:n], in0=v[:, 2:2 + n], scalar=convw[:, 3 * j + 2:3 * j + 3], in1=a1[:, 0:n],
                    op0=ALU.mult, op1=ALU.add),
                    reads=[K(1), K(5), "convw"], writes=[K(4)])
                P.op("pool", lambda e, a0=a0, gg=gg, j=j: e.tensor_tensor(out=yT[:, j, col0:col0 + n], in0=a0[:, 0:n], in1=gg[:, 0:n],
                                                                          op=ALU.mult),
                     reads=[K(4), K(3)], writes=[("yT", j, t_) for t_ in range(4)])

        def out_tile(ci, pids, t, banks, store=False):
            b = ci % 2
            slot0 = need(ci, pids[0])
            slot1 = (slot0 + 1) % 4
            Ws = [WR(slot0).rearrange("p (a b) -> p a b", a=4), WR(slot1).rearrange("p (a b) -> p a b", a=4)]
            for half in range(2):
                bank = banks[half]

                def mm(e, half=half, bank=bank):
                    r = None
                    for j in range(8):
                        r = e.matmul(out=ps[bank][:, :], lhsT=yT[:, j, t * 128:(t + 1) * 128],
                                     rhs=Ws[j // 4][:, j % 4, half * 512:(half + 1) * 512],
                                     start=(j == 0), stop=(j == 7))
                    return r
                P.op("pe", mm, reads=[("wr", slot0), ("wr", slot1)] + [("yT", j, t) for j in range(8)],
                     writes=[("ps", bank)])
                P.op("dve", lambda e, half=half, bank=bank: e.tensor_tensor(
                    out=xb[b][:, t, half * 512:(half + 1) * 512], in0=xb[b][:, t, half * 512:(half + 1) * 512],
                    in1=ps[bank][:, :], op=ALU.add),
                    reads=[("xb", b, t), ("ps", bank)], writes=[("xb", b, t)])
            if store:
                r0 = (ci - 1) * T + t * 128
                P.dma("sp", lambda e: [e.dma_start(out=out_d[r0:r0 + 128, :], in_=xb[b][:, t, :])],
                      ("ost", b, t), reads=[("xb", b, t)])

        def out_parts(ci, pids, t, banks, store=False):
            b = ci % 2
            slot0 = need(ci, pids[0])
            slot1 = (slot0 + 1) % 4
            Ws = [WR(slot0).rearrange("p (a b) -> p a b", a=4), WR(slot1).rearrange("p (a b) -> p a b", a=4)]
            parts = []
            for half in range(2):
                for jh in range(2):
                    def part(half=half, jh=jh):
                        bank = banks[half]

                        def mm(e):
                            r = None
                            for j in range(4 * jh, 4 * jh + 4):
                                r = e.matmul(out=ps[bank][:, :], lhsT=yT[:, j, t * 128:(t + 1) * 128],
                                             rhs=Ws[j // 4][:, j % 4, half * 512:(half + 1) * 512],
                                             start=(j == 0), stop=(j == 7))
                            return r
                        P.op("pe", mm, reads=[("wr", slot0 if jh == 0 else slot1)] + [("yT", j, t) for j in range(4 * jh, 4 * jh + 4)],
                             writes=[("ps", bank)])
                        if jh == 1:
                            P.op("dve", lambda e: e.tensor_tensor(
                                out=xb[b][:, t, half * 512:(half + 1) * 512], in0=xb[b][:, t, half * 512:(half + 1) * 512],
                                in1=ps[bank][:, :], op=ALU.add),
                                reads=[("xb", b, t), ("ps", bank)], writes=[("xb", b, t)])
                            if store and half == 1:
                                r0 = (ci - 1) * T + t * 128
                                P.dma("sp", lambda e: [e.dma_start(out=out_d[r0:r0 + 128, :], in_=xb[b][:, t, :])],
                                      ("ost", b, t), reads=[("xb", b, t)])
                    parts.append(part)
            return parts

        def out_pair(ci, pids, t0):
            b = ci % 2
            slot0 = need(ci, pids[0])
            slot1 = (slot0 + 1) % 4
            Ws = [WR(slot0).rearrange("p (a b) -> p a b", a=4), WR(slot1).rearrange("p (a b) -> p a b", a=4)]
            groups = [(t0 + dt, half, 4 + 2 * dt + half) for dt in range(2) for half in range(2)]

            def mm1(e):
                r = None
                for (t, half, bank) in groups:
                    for j in range(7):
                        r = e.matmul(out=ps[bank][:, :], lhsT=yT[:, j, t * 128:(t + 1) * 128],
                                     rhs=Ws[j // 4][:, j % 4, half * 512:(half + 1) * 512], start=(j == 0), stop=False)
                return r
            P.op("pe", mm1, reads=[("wr", slot0), ("wr", slot1)] + [("yT", j, t0 + dt) for j in range(7) for dt in range(2)],
                 writes=[("ps", bk) for (_, _, bk) in groups])
            for (t, half, bank) in groups:
                P.op("pe", lambda e, t=t, half=half, bank=bank: e.matmul(
                    out=ps[bank][:, :], lhsT=yT[:, 7, t * 128:(t + 1) * 128],
                    rhs=Ws[1][:, 3, half * 512:(half + 1) * 512], start=False, stop=True),
                    reads=[("wr", slot1), ("yT", 7, t)], writes=[("ps", bank)])
                P.op("dve", lambda e, t=t, half=half, bank=bank: e.tensor_tensor(
                    out=xb[b][:, t, half * 512:(half + 1) * 512], in0=xb[b][:, t, half * 512:(half + 1) * 512],
                    in1=ps[bank][:, :], op=ALU.add),
                    reads=[("xb", b, t), ("ps", bank)], writes=[("xb", b, t)])

        def chain_a(src_bank, n, cset):
            P.op("dve", lambda e: e.tensor_copy(out=qs[cset][:, 0, 0:n], in_=ps[src_bank][:, 0:n]),
                 reads=[("ps", src_bank)], writes=[("qs", cset, 0)])
            P.op("act", lambda e: e.activation(out=qs[cset][:, 1, 0:n], in_=ps[src_bank][:, 0:n], func=AF.Square),
                 reads=[("ps", src_bank)], writes=[("qs", cset, 1)])

        def chain_b(n, cset, ss_bank, rq_bank):
            P.op("pe", lambda e: e.matmul(out=ps[ss_bank][:, 0:n], lhsT=bdm, rhs=qs[cset][:, 1, 0:n], start=True, stop=True),
                 reads=[("qs", cset, 1), "cmat_b"], writes=[("ps", ss_bank)])
            P.op("pe", lambda e: e.matmul(out=ps[rq_bank][:, 0:n], lhsT=rotm, rhs=qs[cset][:, 0, 0:n], start=True, stop=True),
                 reads=[("qs", cset, 0), "cmat_b"], writes=[("ps", rq_bank)])

        def chain_c(src_bank, n, cset, ss_bank, rq_bank, gcol, b, dst_ap, dst_key, sset=0, col0=0):
            so = 4 * sset
            rt, rstd, tt, uu = [sc[so + i] for i in range(4)]
            K = lambda i: ("sc", so + i)
            P.op("act", lambda e: e.activation(out=rt[:, 0:n], in_=ps[ss_bank][:, 0:n], func=AF.Ln, scale=1.0 / 64, bias=EPS),
                 reads=[("ps", ss_bank)], writes=[K(0)])
            P.op("act", lambda e: e.activation(out=rstd[:, 0:n], in_=rt[:, 0:n], func=AF.Exp, scale=-0.5),
                 reads=[K(0)], writes=[K(1)])
            P.op("dve", lambda e: e.scalar_tensor_tensor(out=tt[:, 0:n], in0=ps[src_bank][:, 0:n], scalar=qkg[:, gcol:gcol + 1],
                                                         in1=cs[b][:, 0, col0:col0 + n], op0=ALU.mult, op1=ALU.mult),
                 reads=[("ps", src_bank), "qkg", ("cs", b)], writes=[K(2)])
            P.op("dve", lambda e: e.scalar_tensor_tensor(out=uu[:, 0:n], in0=ps[rq_bank][:, 0:n], scalar=qkg[:, gcol + 1:gcol + 2],
                                                         in1=cs[b][:, 1, col0:col0 + n], op0=ALU.mult, op1=ALU.mult),
                 reads=[("ps", rq_bank), "qkg", ("cs", b)], writes=[K(3)])
            P.op("pool", lambda e: e.tensor_tensor(out=tt[:, 0:n], in0=tt[:, 0:n], in1=uu[:, 0:n], op=ALU.add),
                 reads=[K(2), K(3)], writes=[K(2)])
            if isinstance(dst_ap, tuple):
                for hh, dap in enumerate(dst_ap):
                    P.op("pool", lambda e, hh=hh, dap=dap: e.tensor_tensor(out=dap, in0=tt[64 * hh:64 * hh + 64, 0:n],
                                                                         in1=rstd[64 * hh:64 * hh + 64, 0:n], op=ALU.mult),
                         reads=[K(2), K(1)], writes=dst_key if isinstance(dst_key, list) else [dst_key])
            else:
                P.op("pool", lambda e: e.tensor_tensor(out=dst_ap, in0=tt[:, 0:n], in1=rstd[:, 0:n], op=ALU.mult),
                     reads=[K(2), K(1)], writes=dst_key if isinstance(dst_key, list) else [dst_key])

        def stage_kvq(ci, pre_last=None):
            tok0, _n = chunk_geom(ci)
            tiles, col0, n = cgeo(ci)
            nt = len(tiles)
            b = ci % 2
            blk0 = tok0 // 128
            htk1 = [k_ for t_ in tiles for k_ in (("hT", 1, t_), ("hT", 1, t_, "b"))]
            slot = need(ci, PID_V)
            Wv = WR(slot)[:, 0:2048].rearrange("p (a b) -> p a b", a=8)
            for t in tiles:
                if t == tiles[-1] and pre_last is not None:
                    pre_last()
                bank = 4 + t
                c0 = 0

                def mm(e, t=t, bank=bank, c0=c0):
                    r = None
                    for kc in range(8):
                        r = e.matmul(out=ps[bank][:, c0:c0 + 256], lhsT=hT[1][:, kc, t * 128:(t + 1) * 128], rhs=Wv[:, kc, :],
                                     start=(kc == 0), stop=(kc == 7))
                    return r
                P.op("pe", mm, reads=[("wr", slot), ("hT", 1, t), ("hT", 1, t, "b")], writes=[("ps", bank)])
                vs = (blk0 + t) % 8
                src = ps[bank][:, c0:c0 + 256].rearrange("p (k d) -> p k d", k=4)
                P.op("act", lambda e, vs=vs, src=src: e.activation(out=Vp[vs][:, :, 0, 0:64], in_=src, func=AF.Copy),
                     reads=[("ps", bank)], writes=[("Vp", vs, 0)])
                P.op("dve", lambda e, vs=vs, src=src: e.tensor_copy(out=Vp[vs][:, :, 1, 64:128], in_=src),
                     reads=[("ps", bank)], writes=[("Vp", vs, 1)])
            if int(os.environ.get("K_SUB", "9")) <= 1:
                return
            ks = 0 if ci == 0 else ci % 2
            ulist = [("k", c) for c in range(2)]
            if ci > 0:
                ulist += [("q", c) for c in range(8)]
            slots = {}

            def pe_unit(u, kind, idx):
                if kind == "k":
                    if "k" not in slots:
                        slots["k"] = need(ci, PID_K)
                    slot = slots["k"]
                    Wk = WR(slot)[:, 2048:4096].rearrange("p (a b) -> p a b", a=8)
                    lhs = lambda kc: Wk[:, kc, idx * 128:(idx + 1) * 128]
                else:
                    h = idx // 4
                    if ("q", h) not in slots:
                        slots[("q", h)] = need(ci, PID_BQ[h])
                    slot = slots[("q", h)]
                    W = WR(slot).rearrange("p (a b) -> p a b", a=8)
                    lhs = lambda kc: W[:, kc, (idx % 4) * 128:(idx % 4 + 1) * 128]
                bank = u % 3

                def mm(e):
                    r = None
                    for kc in range(8):
                        r = e.matmul(out=ps[bank][:, 0:n], lhsT=lhs(kc), rhs=hT[1][:, kc, col0:col0 + n],
                                     start=(kc == 0), stop=(kc == 7))
                    return r
                P.op("pe", mm, reads=[("wr", slot)] + htk1, writes=[("ps", bank)])

            def fin_unit(u, kind, idx):
                cset = u % 2
                chain_b(n, cset, 3 + cset, 5 + cset)
                if kind == "k":
                    chain_c(u % 3, n, cset, 3 + cset, 5 + cset, 2, b,
                            (KTz[0:64, 0, 2 * idx, ks, col0:col0 + n], KTz[64:128, 1, 2 * idx + 1, ks, col0:col0 + n]),
                            [("KTn", ks, idx), ("KT", ks, 2 * idx), ("KT", ks, 2 * idx + 1)], sset=u % 3, col0=col0)
                    late[u + 3] = (lambda idx=idx: k_swap(idx))
                else:
                    chain_c(u % 3, n, cset, 3 + cset, 5 + cset, 0, b, QT[:, idx, 0:n], ("QT", idx), sset=u % 3)

            late = {}

            def k_swap(c):
                def mm(e):
                    e.matmul(out=ps[7][:, 0:n], lhsT=swapm, rhs=KTz[:, 0, 2 * c, ks, col0:col0 + n], start=True, stop=False)
                    return e.matmul(out=ps[7][:, 0:n], lhsT=swapm, rhs=KTz[:, 1, 2 * c + 1, ks, col0:col0 + n],
                                    start=False, stop=True)
                P.op("pe", mm, reads=[("KTn", ks, c), "cmat_b"], writes=[("ps", 7)])
                P.op("act", lambda e: e.activation(out=KTz[64:128, 1, 2 * c, ks, col0:col0 + n], in_=ps[7][64:128, 0:n],
                                                   func=AF.Copy),
                     reads=[("ps", 7)], writes=[("KT", ks, 2 * c)])
                P.op("act", lambda e: e.activation(out=KTz[0:64, 0, 2 * c + 1, ks, col0:col0 + n], in_=ps[7][0:64, 0:n],
                                                   func=AF.Copy),
                     reads=[("ps", 7)], writes=[("KT", ks, 2 * c + 1)])

            for u, (kind, idx) in enumerate(ulist):
                pe_unit(u, kind, idx)
                chain_a(u % 3, n, u % 2)
                if u >= 1:
                    fin_unit(u - 1, *ulist[u - 1])
                if u in late:
                    late.pop(u)()
            fin_unit(len(ulist) - 1, *ulist[-1])
            for u in sorted(late):
                late.pop(u)()
            if ci == 0:
                return
            zb = [7, 0, 1, 2]
            for c in range(8):
                h = c // 4
                slot = need(ci, PID_BZ[h])
                W = WR(slot).rearrange("p (a b) -> p a b", a=8)
                bank = zb[c % 4]

                def mm(e, W=W, c=c, bank=bank):
                    r = None
                    for kc in range(8):
                        r = e.matmul(out=ps[bank][:, 0:n], lhsT=W[:, kc, (c % 4) * 128:(c % 4 + 1) * 128],
                                     rhs=hT[1][:, kc, 0:n], start=(kc == 0), stop=(kc == 7))
                    return r
                P.op("pe", mm, reads=[("wr", slot)] + HT(1)[:2 * nt], writes=[("ps", bank)])
                sg = sc[4 * (c % 3)]
                P.op("act", lambda e, c=c, bank=bank, sg=sg: e.activation(out=sg[:, 0:n], in_=ps[bank][:, 0:n], func=AF.Sigmoid),
                     reads=[("ps", bank)], writes=[("sc", 4 * (c % 3))])
                P.op("dve", lambda e, c=c, bank=bank, sg=sg: e.tensor_tensor(out=szT[:, c, 0:n], in0=sg[:, 0:n], in1=ps[bank][:, 0:n],
                                                                          op=ALU.mult),
                     reads=[("ps", bank), ("sc", 4 * (c % 3))], writes=[("szT", c)])

        def stage_att(ci):
            tok0, n = chunk_geom(ci)
            nq = n // 128
            blk0 = tok0 // 128
            items = [(qb, kv) for qb in range(nq) for kv in range(4)]

            def S(i):
                qb, kv = items[i]
                blk = blk0 + qb
                st_ = i % 2
                kls = [kslot(blk - 1), kslot(blk)]

                def mm(e, srange=(0, 1)):
                    r = None
                    for bi in range(2):
                        ksl, kof = kls[bi]
                        for s in srange:
                            if F_S256:
                                r = e.matmul(out=ps[2 * st_ + s][:, bi * 256:(bi + 1) * 256],
                                             lhsT=KTz[:, s, kv, ksl, kof:kof + 128],
                                             rhs=QT[:, 2 * kv:2 * kv + 2, qb * 128:(qb + 1) * 128],
                                             start=True, stop=True)
                            else:
                                for gi in range(2):
                                    r = e.matmul(out=ps[2 * st_ + s][:, (bi * 2 + gi) * 128:(bi * 2 + gi + 1) * 128],
                                                 lhsT=KTz[:, s, kv, ksl, kof:kof + 128],
                                                 rhs=QT[:, 2 * kv + gi, qb * 128:(qb + 1) * 128],
                                                 start=True, stop=True)
                    return r
                reads = [("KT", kls[0][0], kv), ("KT", kls[1][0], kv), ("QT", 2 * kv), ("QT", 2 * kv + 1)]
                for s_ in range(2):
                    P.op("pe", lambda e, s_=s_: mm(e, (s_,)), reads=reads, writes=[("ps", 2 * st_ + s_)])

            def EM(i):
                qb, kv = items[i]
                blk = blk0 + qb
                st_ = i % 2
                mset = 0 if blk == 2 else 1
                mk = masks_b[:, 2 * mset:2 * mset + 2, :].unsqueeze(2).to_broadcast([128, 2, 2, 128])
                for s in range(2):
                    P.op("act", lambda e, s=s: e.activation(out=PT[st_][:, s, :], in_=ps[2 * st_ + s][:, :], func=AF.Exp,
                                                            scale=0.125),
                         reads=[("ps", 2 * st_ + s)], writes=[("PT", st_, s)])
                    pv = PT[st_][:, s, :].rearrange("p (a g q) -> p a g q", a=2, g=2)
                    P.op("dve" if F_MASKDVE else "pool", lambda e, pv=pv: e.tensor_tensor(out=pv, in0=pv, in1=mk, op=ALU.mult),
                         reads=[("PT", st_, s), "masks_b"], writes=[("PT", st_, s)])

            def PV(i):
                qb, kv = items[i]
                blk = blk0 + qb
                st_ = i % 2
                bank = 4 + st_
                vss = [(blk - 1) % 8, blk % 8]

                def mm(e):
                    r = None
                    k = 0
                    for s in range(2):
                        for bi in range(2):
                            r = e.matmul(out=ps[bank][:, 0:256], lhsT=Vp[vss[bi]][:, kv, s, :],
                                         rhs=PT[st_][:, s, bi * 256:(bi + 1) * 256], start=(k == 0), stop=(k == 3))
                            k += 1
                    r = e.matmul(out=ps[bank][:, 256:512], lhsT=sinkmat[:, kv * 128:(kv + 1) * 128], rhs=selmat[:, :],
                                 start=True, stop=False)
                    k = 0
                    for s in range(2):
                        for bi in range(2):
                            r = e.matmul(out=ps[bank][:, 256:512], lhsT=cmat_b[:, 3 + s, :],
                                         rhs=PT[st_][:, s, bi * 256:(bi + 1) * 256], start=False, stop=(k == 3))
                            k += 1
                    return r
                P.op("pe", mm, reads=[("PT", st_, 0), ("PT", st_, 1)] +
                     [("Vp", v_, h_) for v_ in vss for h_ in range(2)] +
                     ["sinkmat", "selmat", "cmat_b"], writes=[("ps", bank)])

            def NORM(i):
                qb, kv = items[i]
                st_ = i % 2
                bank = 4 + st_
                so = 6 * st_ + 4
                rD, tt = sc[so], sc[so + 1]
                P.op("act", lambda e: e.activation(out=rD[:, 0:256], in_=ps[bank][:, 256:512], func=AF.Ln),
                     reads=[("ps", bank)], writes=[("sc", so)])
                P.op("act", lambda e: e.activation(out=rD[:, 256:512], in_=rD[:, 0:256], func=AF.Exp, scale=-1.0),
                     reads=[("sc", so)], writes=[("sc", so)])
                P.op("dve", lambda e: e.tensor_tensor(out=tt[:, 0:256], in0=rD[:, 256:512], in1=ps[bank][:, 0:256], op=ALU.mult),
                     reads=[("ps", bank), ("sc", so)], writes=[("sc", so + 1)])
                P.op("pool", lambda e: e.tensor_tensor(out=yT[:, 2 * kv:2 * kv + 2, qb * 128:(qb + 1) * 128],
                                                       in0=tt[:, 0:256].rearrange("p (c q) -> p c q", c=2),
                                                       in1=szT[:, 2 * kv:2 * kv + 2, qb * 128:(qb + 1) * 128], op=ALU.mult),
                     reads=[("sc", so + 1), ("szT", 2 * kv), ("szT", 2 * kv + 1)], writes=[("yT", 2 * kv, qb), ("yT", 2 * kv + 1, qb)])

            pending = []

            def norm_and_out(j):
                NORM(j)
                if j % 4 == 3:
                    pending.extend(out_parts(ci, PID_BOUT, j // 4, (6, 7), store=True))

            ni = len(items)
            for i in range(ni):
                S(i)
                if pending:
                    pending.pop(0)()
                EM(i)
                if i >= 1:
                    PV(i - 1)
                if i >= 2:
                    norm_and_out(i - 2)
            PV(ni - 1)
            if ni >= 2:
                norm_and_out(ni - 2)
            norm_and_out(ni - 1)
            while pending:
                pending.pop(0)()

        def store_out(ci):
            b = ci % 2
            r0 = (ci - 1) * T
            P.dma("sp", lambda e: [e.dma_start(out=out_d[r0:r0 + T, :].rearrange("(t p) d -> p t d", p=128), in_=xb[b][:])],
                  ("ost", b), reads=XB(b))

        load_x(0)
        bg_load(0)
        for t in range(HALO // 128):
            norm_tile(0, 0, t)
        for ci in range(n_chunks + 1):
            tok0, n = chunk_geom(ci)
            nt = n // 128
            has_next = ci + 1 <= n_chunks
            if has_next:
                load_x(ci + 1)
            hN = 0 if (ci + 1) % 2 == 0 else 2

            def inter(j, ci=ci, hN=hN, has_next=has_next):
                if ci == 0 and j % 2 == 1:
                    bg_step(j // 2)
                if ci == 0:
                    j0 = 3
                    if has_next and j0 <= j <= j0 + 3:
                        norm_a(ci + 1, hN, j - j0)
                    if has_next and j0 + 1 <= j <= j0 + 4:
                        norm_b(ci + 1, hN, j - j0 - 1, base=4)
                elif has_next:
                    if j == 2:
                        norm_stats_all(ci + 1)
                        norm_scale(ci + 1, 0)
                        norm_scale(ci + 1, 1)
                    if j == 3:
                        norm_b(ci + 1, hN, 0, base=4)
                        norm_scale(ci + 1, 2)
                    if j == 4:
                        norm_b(ci + 1, hN, 1, base=4)
                        norm_scale(ci + 1, 3)
                    if j == 5:
                        norm_b(ci + 1, hN, 2, base=4)
                    if j == 6:
                        norm_b(ci + 1, hN, 3, base=4)
                if ci == 1:
                    bg_step(4 + j)
            stage_a_in(ci, inter)
            if ci == 0:
                out_tile(0, PID_AOUT, 1, (4, 5))
                norm_a(0, 1, 1)
                last_nb = (lambda: norm_b(0, 1, 1))
            else:
                out_pair(ci, PID_AOUT, 0)
                norm_a(ci, 1, 0)
                norm_a(ci, 1, 1)
                for t in range(2, nt):
                    out_tile(ci, PID_AOUT, t, (4 + 2 * (t % 2), 5 + 2 * (t % 2)))
                    norm_b(ci, 1, t - 2, dve_only=True)
                    norm_a(ci, 1, t)
                for t in range(max(nt - 2, 0), nt - 1):
                    norm_b(ci, 1, t, dve_only=True)
                last_nb = (lambda ci=ci, nt=nt: norm_b(ci, 1, nt - 1, dve_only=True))
            if dbg and ci == 1:
                P.dma("sp", lambda e: [e.dma_start(out=dbg_x1.rearrange("(t p) d -> p t d", p=128), in_=xb[1][:])],
                      "dbg", reads=XB(1))
            stage_kvq(ci, last_nb)
            if ci == 0:
                continue
            stage_att(ci)
            if dbg and ci == 1:
                P.dma("sp", lambda e: [e.dma_start(out=dbg_kt, in_=KTz[:, 0, :, :, :].rearrange("p a b c -> p (a b c)")),
                                       e.dma_start(out=dbg_qt, in_=QT[:].rearrange("p a b -> p (a b)")),
                                       e.dma_start(out=dbg_yt, in_=yT[:].rearrange("p a b -> p (a b)"))],
                      "dbg2", ndma=3, reads=[("KT", s_, k_) for s_ in range(2) for k_ in range(4)] +
                      [("QT", c) for c in range(8)] + [("yT", c, t_) for c in range(8) for t_ in range(4)])
        stats = P.emit()
        print("ops", len(P.ops), "signals/waits", stats)
    return nc


def _const_mats():
    ident = np.eye(128, dtype=np.float32)
    bd = np.zeros((128, 128), np.float32)
    bd[:64, :64] = 1.0
    bd[64:, 64:] = 1.0
    rot = np.zeros((128, 128), np.float32)
    for base in (0, 64):
        for i in range(8):
            rot[base + i + 8, base + i] = -1.0
            rot[base + i, base + i + 8] = 1.0
    elo = np.zeros((128, 128), np.float32)
    elo[:, :64] = 1.0
    ehi = np.zeros((128, 128), np.float32)
    ehi[:, 64:] = 1.0
    swap = np.zeros((128, 128), np.float32)
    for k in range(128):
        swap[k, (k + 64) % 128] = 1.0
    return np.ascontiguousarray(np.stack([ident, bd, rot, elo, ehi, swap], axis=1))


def _rope_table(pos):
    inv = (np.float32(ROPE_THETA) ** (-np.arange(0, 16, 2, dtype=np.float32) / np.float32(16))).astype(np.float32)
    ang = (pos.astype(np.float32)[:, None] * inv[None, :]).astype(np.float32)
    cos = np.cos(ang).astype(np.float32).T
    sin = np.sin(ang).astype(np.float32).T
    C = np.ones((64, len(pos)), np.float32)
    S = np.zeros((64, len(pos)), np.float32)
    C[0:8] = cos
    C[8:16] = cos
    S[0:8] = sin
    S[8:16] = sin
    C = np.concatenate([C, C], 0)
    S = np.concatenate([S, S], 0)
    return np.ascontiguousarray(np.stack([C, S], axis=1))


def _perm64(g):
    gp = g.copy()
    gp[0:8] = g[8:16]
    gp[8:16] = g[0:8]
    return gp


def make_in_maps(x, ln_a, w_in_a, conv_a, w_out_a, ln_kv, w_kv, k_norm, ln_b, w_in_b, q_norm, sinks, w_out_b,
                 starts, n_chunks):
    f = lambda a: np.ascontiguousarray(np.asarray(a, dtype=np.float32))
    x = f(x)
    gains = np.concatenate([f(ln_a)[0].reshape(8, 128).T, f(ln_kv).reshape(8, 128).T, f(ln_b)[0].reshape(8, 128).T], axis=1)
    convw = f(conv_a)[0, :, 0, :].reshape(3, 8, 128).transpose(2, 1, 0).reshape(128, 24)
    gq = f(q_norm)[0]
    gk = f(k_norm)
    qkg = np.stack([np.tile(gq, 2), np.tile(_perm64(gq), 2), np.tile(gk, 2), np.tile(_perm64(gk), 2)], axis=1)
    sinkrow = np.repeat(f(sinks)[0], 64).reshape(4, 2, 128).transpose(1, 0, 2).reshape(2, 512)
    sel = np.zeros((2, 256), np.float32)
    sel[0, 0:128] = 1.0
    sel[1, 128:256] = 1.0
    cmat = _const_mats()
    ci_ = np.arange(128)[:, None]
    ai_ = np.arange(128)[None, :]
    m_own = (ci_ <= ai_).astype(np.float32)
    m_prev = (ci_ > ai_).astype(np.float32)
    shared = dict(w_in_a=f(w_in_a)[0], w_out_a=f(w_out_a)[0], w_kv=f(w_kv), w_in_b=f(w_in_b)[0], w_out_b=f(w_out_b)[0],
                  gains=np.ascontiguousarray(gains), convw=np.ascontiguousarray(convw), qkg=np.ascontiguousarray(qkg),
                  sink2=np.ascontiguousarray(sinkrow), sel=sel, cmat=cmat)
    maps = []
    ntok = n_chunks * T
    for (bi, s0) in starts:
        xs = np.zeros((HALO + ntok, D), np.float32)
        if s0 >= HALO:
            xs[:] = x[bi, s0 - HALO:s0 + ntok]
            m_first = m_prev
        else:
            assert s0 == 0
            xs[HALO:] = x[bi, 0:ntok]
            m_first = np.zeros_like(m_prev)
        pos = np.maximum(np.arange(s0 - HALO, s0 + ntok), 0)
        masks = np.ascontiguousarray(np.stack([m_first, m_own, m_prev, m_own], axis=1))
        m = dict(shared)
        m.update(x=xs, cst=_rope_table(pos), masks=masks)
        maps.append(m)
    return maps


_NC_CACHE = {}


def kernel(x, ln_a, w_in_a, conv_a, w_out_a, ln_kv, w_kv, k_norm, ln_b, w_in_b, q_norm, sinks, w_out_b):
    n_chunks = TOK_CORE // T
    starts = [(c // 2, (c % 2) * TOK_CORE) for c in range(NCORE)]
    maps = make_in_maps(x, ln_a, w_in_a, conv_a, w_out_a, ln_kv, w_kv, k_norm, ln_b, w_in_b, q_norm, sinks, w_out_b,
                        starts, n_chunks)
    if "nc" not in _NC_CACHE:
        _NC_CACHE["nc"] = build_nc(n_chunks)
    res = run_bass_kernel_spmd(_NC_CACHE["nc"], maps, core_ids=list(range(NCORE)))
    out = np.empty((4, SEQ, D), np.float32)
    for c, (bi, s0) in enumerate(starts):
        out[bi, s0:s0 + TOK_CORE] = res.results[c]["out"]
    return out
```
